# Optimizing a Trainium2 kernel written in Bass

```python
import math
import jax, jax.numpy as jnp
from jax import lax
import numpy as np

D_MODEL = 1024
BATCH = 8
SEQ = 4096
DEPTH = 1
DEC_BATCH = 8
DEC_SEQ = 2048
PAST_LEN = 128

HEAD_DIM = 64
A_HEADS = 8
A_WIDTH = A_HEADS * HEAD_DIM
B_HEADS = 4
B_QK_DIM = 64
B_V_DIM = 2 * B_QK_DIM
B_WIDTH = B_HEADS * B_V_DIM
MIX_WIDTH = A_WIDTH + B_WIDTH
IN_COLS = 3 * A_WIDTH + 3 * B_WIDTH
D_FF = 2816
ROPE_THETA = 10000.0
EPS = 1e-6
NEG_INF = -1e30
DILATED_PATTERNS = ((128, 1), (512, 4), (2048, 16))
LOCAL_BLOCK = 64
Q_BLOCK_DENSE = 128

kernel_name = "hymba_dilated_diff_macaron_encoder"


def rms_norm(x, g):
    xf = x.astype(jnp.float32)
    y = xf * lax.rsqrt(jnp.mean(xf * xf, axis=-1, keepdims=True) + EPS)
    return (y * g.astype(jnp.float32)).astype(x.dtype)


def rope_tables(seq, dim):
    inv = 1.0 / (ROPE_THETA ** (jnp.arange(0, dim, 2, dtype=jnp.float32) / dim))
    ang = jnp.arange(seq, dtype=jnp.float32)[:, None] * inv[None, :]
    ang = jnp.concatenate([ang, ang], axis=-1)
    return jnp.cos(ang), jnp.sin(ang)


def apply_rope(x, cos, sin):
    shape = (x.shape[1],) + (1,) * (x.ndim - 3) + (x.shape[-1],)
    c = cos.reshape(shape)
    s = sin.reshape(shape)
    x1, x2 = jnp.split(x, 2, axis=-1)
    rot = jnp.concatenate([-x2, x1], axis=-1)
    return (x * c + rot * s).astype(x.dtype)


def swiglu(x, w_gu, w_down):
    g, u = jnp.split(x @ w_gu, 2, axis=-1)
    return (jax.nn.silu(g) * u) @ w_down


def dilated_window_attention(q, k, v, dilation, half):
    B, S, H, D = q.shape
    L = S // dilation
    qs = q.reshape(B, L, dilation, H, D)
    ks = k.reshape(B, L, dilation, H, D)
    vs = v.reshape(B, L, dilation, H, D)
    bq = math.gcd(L, LOCAL_BLOCK)
    nb = L // bq
    kw = bq + 2 * half
    pad = ((0, 0), (half, half), (0, 0), (0, 0), (0, 0))
    kp = jnp.pad(ks, pad)
    vp = jnp.pad(vs, pad)
    idx = (jnp.arange(nb) * bq)[:, None] + jnp.arange(kw)[None, :]
    kb = kp[:, idx]
    vb = vp[:, idx]
    qb = qs.reshape(B, nb, bq, dilation, H, D)
    s = jnp.einsum('bnqrhd,bnkrhd->bnrhqk', qb, kb,
                   preferred_element_type=jnp.float32) * (D ** -0.5)
    rel = jnp.arange(kw)[None, :] - half - jnp.arange(bq)[:, None]
    kpos = idx - half
    valid = (jnp.abs(rel) <= half)[None] & ((kpos >= 0) & (kpos < L))[:, None, :]
    s = jnp.where(valid[None, :, None, None], s, NEG_INF)
    m = jnp.max(s, axis=-1, keepdims=True)
    p = jnp.exp(s - m)
    l = jnp.sum(p, axis=-1, keepdims=True)
    o = jnp.einsum('bnrhqk,bnkrhd->bnqrhd', p, vb.astype(jnp.float32))
    o = o / jnp.transpose(l, (0, 1, 4, 2, 3, 5))
    lse = (m + jnp.log(l))[..., 0]
    lse = jnp.transpose(lse, (0, 1, 4, 2, 3)).reshape(B, S, H)
    return o.reshape(B, S, H, D), lse


def diff_attention(q, k, v, lam):
    B, S, H, _, D = q.shape
    nblk = S // Q_BLOCK_DENSE
    qb = jnp.transpose(q.reshape(B, nblk, Q_BLOCK_DENSE, H, 2, D), (1, 0, 2, 3, 4, 5))

    def one_block(qblk):
        s = jnp.einsum('bqhcd,bkhcd->bhcqk', qblk, k,
                       preferred_element_type=jnp.float32) * (D ** -0.5)
        p = jax.nn.softmax(s, axis=-1)
        a = p[:, :, 0] - lam * p[:, :, 1]
        return jnp.einsum('bhqk,bkhe->bqhe', a.astype(v.dtype), v)

    o = lax.map(one_block, qb)
    return jnp.transpose(o, (1, 0, 2, 3, 4)).reshape(B, S, H, 2 * D)


def encoder_layer(x, layer, g_ffn1, w_ffn1_gu, w_ffn1_down, g_mix, w_in,
                  g_a_q, g_a_k, g_a_out, g_b_q, g_b_k,
                  lam_q1, lam_k1, lam_q2, lam_k2, g_b_out, w_out,
                  g_ffn2, w_ffn2_gu, w_ffn2_down, g_final):
    B, S, _ = x.shape
    cos, sin = rope_tables(S, HEAD_DIM)
    x = x + 0.5 * swiglu(rms_norm(x, g_ffn1), w_ffn1_gu, w_ffn1_down)
    h = rms_norm(x, g_mix)
    proj = h @ w_in
    a_q, a_k, a_v, b_q, b_k, b_v = jnp.split(
        proj, [A_WIDTH, 2 * A_WIDTH, 3 * A_WIDTH,
               3 * A_WIDTH + B_WIDTH, 3 * A_WIDTH + 2 * B_WIDTH], axis=-1)
    a_q = apply_rope(rms_norm(a_q.reshape(B, S, A_HEADS, HEAD_DIM), g_a_q), cos, sin)
    a_k = apply_rope(rms_norm(a_k.reshape(B, S, A_HEADS, HEAD_DIM), g_a_k), cos, sin)
    a_v = a_v.reshape(B, S, A_HEADS, HEAD_DIM)
    outs, lses = [], []
    for window, dilation in DILATED_PATTERNS:
        o_p, lse_p = dilated_window_attention(a_q, a_k, a_v, dilation, window // (2 * dilation))
        outs.append(o_p)
        lses.append(lse_p)
    wts = jax.nn.softmax(jnp.stack(lses, axis=0), axis=0)
    o_a = jnp.sum(wts[..., None] * jnp.stack(outs, axis=0), axis=0).astype(x.dtype)
    o_a = rms_norm(o_a, g_a_out)
    lambda_init = 0.8 - 0.6 * math.exp(-0.3 * layer)
    f32 = jnp.float32
    lam = (jnp.exp(jnp.dot(lam_q1.astype(f32), lam_k1.astype(f32)))
           - jnp.exp(jnp.dot(lam_q2.astype(f32), lam_k2.astype(f32))) + lambda_init)
    b_q = apply_rope(rms_norm(b_q.reshape(B, S, B_HEADS, 2, B_QK_DIM), g_b_q), cos, sin)
    b_k = apply_rope(rms_norm(b_k.reshape(B, S, B_HEADS, 2, B_QK_DIM), g_b_k), cos, sin)
    b_v = b_v.reshape(B, S, B_HEADS, B_V_DIM)
    o_b = diff_attention(b_q, b_k, b_v, lam)
    o_b = rms_norm(o_b, g_b_out) * (1.0 - lambda_init)
    mixed = jnp.concatenate([o_a.reshape(B, S, A_WIDTH),
                             o_b.reshape(B, S, B_WIDTH).astype(x.dtype)], axis=-1)
    x = x + mixed @ w_out
    x = x + 0.5 * swiglu(rms_norm(x, g_ffn2), w_ffn2_gu, w_ffn2_down)
    return rms_norm(x, g_final)


def setup_inputs(seed: int = 0) -> dict:
    key = jax.random.key(seed)
    ks = jax.random.split(key, 24)
    f32 = jnp.float32

    def nrm(k, shape, scale):
        return jax.random.normal(k, shape, f32) * scale

    def gain(k, shape):
        return 1.0 + 0.01 * jax.random.normal(k, shape, f32)

    return {
        "x_prompt": jax.random.normal(ks[0], (BATCH, SEQ, D_MODEL), f32),
        "x_sample": jax.random.normal(ks[1], (DEC_BATCH, DEC_SEQ, D_MODEL), f32),
        "g_ffn1": gain(ks[2], (DEPTH, D_MODEL)),
        "w_ffn1_gu": nrm(ks[3], (DEPTH, D_MODEL, 2 * D_FF), D_MODEL ** -0.5),
        "w_ffn1_down": nrm(ks[4], (DEPTH, D_FF, D_MODEL), D_FF ** -0.5),
        "g_mix": gain(ks[5], (DEPTH, D_MODEL)),
        "w_in": nrm(ks[6], (DEPTH, D_MODEL, IN_COLS), D_MODEL ** -0.5),
        "g_a_q": gain(ks[7], (DEPTH, HEAD_DIM)),
        "g_a_k": gain(ks[8], (DEPTH, HEAD_DIM)),
        "g_a_out": gain(ks[9], (DEPTH, A_HEADS, HEAD_DIM)),
        "g_b_q": gain(ks[10], (DEPTH, B_QK_DIM)),
        "g_b_k": gain(ks[11], (DEPTH, B_QK_DIM)),
        "lam_q1": nrm(ks[12], (DEPTH, B_QK_DIM), 0.1),
        "lam_k1": nrm(ks[13], (DEPTH, B_QK_DIM), 0.1),
        "lam_q2": nrm(ks[14], (DEPTH, B_QK_DIM), 0.1),
        "lam_k2": nrm(ks[15], (DEPTH, B_QK_DIM), 0.1),
        "g_b_out": gain(ks[16], (DEPTH, B_V_DIM)),
        "w_out": nrm(ks[17], (DEPTH, MIX_WIDTH, D_MODEL), MIX_WIDTH ** -0.5),
        "g_ffn2": gain(ks[18], (DEPTH, D_MODEL)),
        "w_ffn2_gu": nrm(ks[19], (DEPTH, D_MODEL, 2 * D_FF), D_MODEL ** -0.5),
        "w_ffn2_down": nrm(ks[20], (DEPTH, D_FF, D_MODEL), D_FF ** -0.5),
        "g_final": gain(ks[21], (DEPTH, D_MODEL)),
    }


def reference(x_prompt, x_sample, g_ffn1, w_ffn1_gu, w_ffn1_down, g_mix, w_in,
              g_a_q, g_a_k, g_a_out, g_b_q, g_b_k, lam_q1, lam_k1, lam_q2, lam_k2,
              g_b_out, w_out, g_ffn2, w_ffn2_gu, w_ffn2_down, g_final):
    def trunk(x):
        for l in range(DEPTH):
            x = encoder_layer(x, l, g_ffn1[l], w_ffn1_gu[l], w_ffn1_down[l], g_mix[l], w_in[l],
                              g_a_q[l], g_a_k[l], g_a_out[l], g_b_q[l], g_b_k[l],
                              lam_q1[l], lam_k1[l], lam_q2[l], lam_k2[l], g_b_out[l], w_out[l],
                              g_ffn2[l], w_ffn2_gu[l], w_ffn2_down[l], g_final[l])
        return x

    y_prompt = trunk(x_prompt)
    y_sample = trunk(x_sample)
    return (y_prompt, y_sample)
```

```python
import math
from collections import deque
from contextlib import ExitStack

import numpy as np
import ml_dtypes
import concourse.bass as bass
import concourse.mybir as mybir
from concourse.bass_utils import run_bass_kernel_spmd

F32 = mybir.dt.float32
BF16 = mybir.dt.bfloat16
AF = mybir.ActivationFunctionType
ALU = mybir.AluOpType
AX = mybir.AxisListType

DM = 1024
DFF = 2816
NF = 22
T = 512
EPS = 1e-6
NFILL = 44
NFILL2 = 14
LAMBDA_INIT = 0.8 - 0.6 * math.exp(-0.3 * 0)


class Buf:
    __slots__ = ("name", "w", "r")

    def __init__(self, name):
        self.name = name
        self.w = None
        self.r = {}


class Op:
    __slots__ = ("eng", "emit", "deps", "sig", "seq", "dsem", "dbase", "dend", "idx")


COMPUTE = ("pe", "act", "dve", "pool")
ALLENG = ("pe", "act", "dve", "pool", "sp")


class Sched:
    def __init__(self):
        self.streams = {e: [] for e in ALLENG}
        self.dma_tot = {}
        self.nops = 0
        self.barrier_deps = {}
        self.last_dma = {}

    def add(self, eng, emit, reads=(), writes=(), dma=None, ndma=1):
        op = Op()
        op.eng = eng
        op.emit = emit
        op.sig = False
        op.seq = None
        op.idx = self.nops
        self.nops += 1
        deps = {}
        for b in reads:
            if b.w is not None:
                deps[id(b.w)] = (b.w, True)
        for b in writes:
            if b.w is not None:
                deps[id(b.w)] = (b.w, True)
            for r in b.r.values():
                if id(r) not in deps:
                    deps[id(r)] = (r, False)
        bd = self.barrier_deps.pop(eng, None)
        if bd:
            for d in bd:
                deps[id(d)] = (d, True)
        if dma is not None:
            op.dsem = dma
            op.dbase = self.dma_tot.get(dma, 0)
            op.dend = op.dbase + 16 * ndma
            self.dma_tot[dma] = op.dend
            self.last_dma[dma] = op
        else:
            op.dsem = None
            op.dbase = op.dend = 0
        final = []
        for d, strong in deps.values():
            if d is op:
                continue
            if d.dsem is None and d.eng == eng:
                if eng == "pe":
                    continue
                if not strong:
                    continue
            final.append(d)
            if d.dsem is None:
                d.sig = True
        op.deps = final
        wset = set(id(b) for b in writes)
        for b in writes:
            b.w = op
            b.r = {}
        key = eng if dma is None else ("dma", dma)
        for b in reads:
            if id(b) not in wset:
                b.r[key] = op
        self.streams[eng].append(op)
        return op

    def barrier(self):
        deps = []
        for e in ALLENG:
            for op in reversed(self.streams[e]):
                if op.dsem is None:
                    deps.append(op)
                    break
        deps.extend(self.last_dma.values())
        for d in deps:
            if d.dsem is None:
                d.sig = True
        for e in ALLENG:
            self.barrier_deps[e] = self.barrier_deps.get(e, []) + deps

    def emit_all(self, nc):
        for e in COMPUTE:
            c = 0
            for op in self.streams[e]:
                if op.sig and op.dsem is None:
                    c += 1
                    op.seq = c
        with ExitStack() as es:
            esem = {e: es.enter_context(nc.semaphore("sem_" + e)) for e in COMPUTE}
            dsem = {n: es.enter_context(nc.semaphore("dsem_" + n)) for n in self.dma_tot}
            block = es.enter_context(nc.Block())
            sched = self

            def run(ename, eng):
                waited = {}

                def wait(key, sem, val):
                    if val <= 0:
                        return
                    if waited.get(key, 0) < val:
                        eng.wait_ge(sem, val)
                        waited[key] = val

                for op in sched.streams[ename]:
                    for d in op.deps:
                        if d.dsem is None:
                            wait(d.eng, esem[d.eng], d.seq)
                        else:
                            wait(("d", d.dsem), dsem[d.dsem], d.dend)
                    if op.dsem is not None:
                        wait(("d", op.dsem), dsem[op.dsem], op.dbase)
                        for ins in op.emit(eng):
                            ins.then_inc(dsem[op.dsem], 16)
                    else:
                        ins = op.emit(eng)
                        if op.sig:
                            ins.then_inc(esem[ename], 1)
                if ename == "sp":
                    for n, tot in sched.dma_tot.items():
                        wait(("d", n), dsem[n], tot)

            @block.tensor
            def _(e):
                run("pe", e)

            @block.scalar
            def _(e):
                run("act", e)

            @block.vector
            def _(e):
                run("dve", e)

            @block.gpsimd
            def _(e):
                run("pool", e)

            @block.sync
            def _(e):
                run("sp", e)


class Ring:
    def __init__(self, S, slots, name):
        self.S = S
        self.slots = slots
        self.n = len(slots)
        self.name = name
        self.pending = deque()
        self.issued = 0
        self.used = 0

    def plan(self, srcs):
        self.pending.extend(srcs)

    def prefetch(self, upto):
        while self.issued < upto and self.pending:
            src, sbuf = self.pending.popleft()
            i = self.issued % self.n
            ap, buf = self.slots[i]
            self.S.add("sp", (lambda e, ap=ap, src=src: [e.dma_start(out=ap, in_=src)]),
                       reads=[sbuf], writes=[buf], dma="%s%d" % (self.name, i))
            self.issued += 1

    def next(self, ahead):
        i = self.used
        self.used += 1
        self.prefetch(i + 1 + ahead)
        assert self.issued > i
        ap, buf = self.slots[i % self.n]
        return ap, buf, i

    def check(self, i):
        assert self.issued <= i + self.n, "ring slot overwritten while live"


def build_program(seqs, debug=False):
    NTOK = sum(seqs)
    nc = bass.Bass("TRN2", target_bir_lowering=False)
    S = Sched()

    def din(name, shape, dt=F32):
        return nc.dram_tensor(name, list(shape), dt, kind="ExternalInput").ap()

    x_d = din("x", [NTOK, DM])
    wgu_d = [din("w_gu1", [DM, 2 * DFF]), din("w_gu2", [DM, 2 * DFF])]
    wd_d = [din("w_d1", [DFF, DM]), din("w_d2", [DFF, DM])]
    win_d = din("w_in", [DM, 3072])
    wout_d = din("w_out", [DM, DM])
    gcols_d = din("gcols", [128, 32])
    gbc_d = din("gbc", [128, 1024 + 512 + 256])
    cs_d = din("cs", [NTOK, 128])
    ident_d = din("ident", [128, 128], BF16)
    amask_d = din("amask", [128, 20 * 512], BF16)
    y_d = nc.dram_tensor("y", [NTOK, DM], F32, kind="ExternalOutput").ap()

    ikind = "ExternalOutput" if debug else "Internal"
    x1s = nc.dram_tensor("x1s", [NTOK, DM], F32, kind=ikind).ap()
    qts = nc.dram_tensor("qts", [16, 128, NTOK], BF16, kind=ikind).ap()
    vs = nc.dram_tensor("vs", [NTOK, 1024], BF16, kind=ikind).ap()
    mixd = nc.dram_tensor("mixd", [8, 128, NTOK], BF16, kind=ikind).ap() if debug else None
    wgus = [nc.dram_tensor("wgus%d" % i, [11, 128, 8 * 512], BF16, kind="Internal").ap() for i in range(2)]
    wds = [nc.dram_tensor("wds%d" % i, [4, 128, 11 * 512], BF16, kind="Internal").ap() for i in range(2)]
    wins = nc.dram_tensor("wins", [6, 128, 8 * 512], BF16, kind="Internal").ap()
    wouts = nc.dram_tensor("wouts", [2, 128, 8 * 512], BF16, kind="Internal").ap()

    ARENA_KB = 192
    arena = nc.alloc_sbuf_tensor("arena", [128, ARENA_KB * 256], F32).ap()

    def carve(off, shape, dt):
        n = 1
        for s in shape[1:]:
            n *= s
        nbytes = n * (4 if dt == F32 else 2)
        assert off % 4 == 0 and nbytes % 4 == 0
        assert off + nbytes <= ARENA_KB * 1024, (off, nbytes)
        v = arena[:, off // 4:(off + nbytes) // 4]
        if dt != F32:
            v = v.bitcast(dt)
        if len(shape) == 3:
            v = v.rearrange("p (a b) -> p a b", b=shape[2])
        elif len(shape) == 4:
            v = v.rearrange("p (a b c) -> p a b c", b=shape[2], c=shape[3])
        return v, off + nbytes

    ident = nc.alloc_sbuf_tensor("sb_ident", [128, 128], BF16).ap()
    ones = nc.alloc_sbuf_tensor("sb_ones", [128, 128], BF16).ap()
    bones = nc.alloc_sbuf_tensor("sb_bones", [128, 128], BF16).ap()
    zfill = nc.alloc_sbuf_tensor("sb_zfill", [128, 512], BF16).ap()
    gcols = nc.alloc_sbuf_tensor("sb_gcols", [128, 32], F32).ap()
    gbc = nc.alloc_sbuf_tensor("sb_gbc", [128, 1792], F32).ap()
    small = nc.alloc_sbuf_tensor("sb_small", [128, 64], F32).ap()
    lamp = nc.alloc_sbuf_tensor("sb_lamp", [128, 128], F32).ap()
    Bconst = Buf("const")
    Bsmall = Buf("small")
    gfin = gbc[:, 0:1024]
    g44 = gbc[:, 1024:1536].rearrange("p (t h d) -> p t h d", h=2, d=64)
    lamv = gbc[:, 1536:1792].rearrange("p (t d) -> p t d", d=64)
    nlam = small[:, 0:1]
    gbs = small[:, 1:2]

    ps_all = nc.alloc_psum_tensor("ps", [128, 4096], F32).ap()
    bank_ap = [ps_all[:, i * 512:(i + 1) * 512] for i in range(8)]
    bank_buf = [Buf("bank%d" % i) for i in range(8)]
    bank_ctr = [0]

    def next_bank():
        i = bank_ctr[0] % 8
        bank_ctr[0] += 1
        return i

    K = 1024
    o = 64 * K
    xt = []
    for i in range(2):
        v, o = carve(o, [128, 4, 1024], F32)
        xt.append(v)
    xt_buf = [[Buf("xt%d_%d" % (i, j)) for j in range(4)] for i in range(2)]
    hb, o = carve(o, [128, 4, 1024], BF16)
    hb_buf = [Buf("hb%d" % j) for j in range(4)]
    hT, o = carve(o, [128, 8, 512], BF16)
    hT_buf = Buf("hT")
    aT, o = carve(o, [128, NF, 512], BF16)
    aT_buf = [Buf("aT%d" % f) for f in range(NF)]
    wr_slots = []
    for i in range(4):
        v, o = carve(o, [128, 8, 512], BF16)
        wr_slots.append((v, Buf("wr%d" % i)))
    wd_slots = []
    for i in range(2):
        v, o = carve(o, [128, 11, 512], BF16)
        wd_slots.append((v, Buf("wd%d" % i)))
    sg = []
    for i in range(2):
        v, o = carve(o, [128, 512], BF16)
        sg.append((v, Buf("sg%d" % i)))
    ffn_end = o
    assert ffn_end <= ARENA_KB * K
    wring = Ring(S, wr_slots, "wr")
    dring = Ring(S, wd_slots, "wd")

    o = 0
    QKst, o = carve(o, [128, 16, 512], BF16)
    QKst_buf = Buf("QKst")
    Vst, o = carve(o, [128, 4, 1024], BF16)
    Vst_buf = Buf("Vst")
    cst = []
    for i in range(2):
        v, o = carve(o, [128, 4, 128], F32)
        cst.append((v, Buf("cst%d" % i)))
    gcs, o = carve(o, [128, 4, 4, 128], F32)
    gcs_buf = Buf("gcs")
    sqb, xnb, t1b, t2b, qkb = [], [], [], [], []
    NQS = 3
    for i in range(NQS):
        v, o = carve(o, [128, 512], F32)
        sqb.append((v, Buf("sqb%d" % i)))
        v, o = carve(o, [128, 512], F32)
        xnb.append((v, Buf("xnb%d" % i)))
        v, o = carve(o, [128, 512], F32)
        t1b.append((v, Buf("t1b%d" % i)))
        v, o = carve(o, [128, 512], F32)
        t2b.append((v, Buf("t2b%d" % i)))
        v, o = carve(o, [128, 512], BF16)
        qkb.append((v, Buf("qkb%d" % i)))
    p1_end = o
    assert p1_end <= 64 * K, p1_end
    o = ffn_end
    for i in range(2):
        v, o = carve(o, [128, 512], BF16)
        qkb.append((v, Buf("qkb%d" % (NQS + i))))
    stat = nc.alloc_sbuf_tensor("sb_stat", [128, 96], F32).ap()
    stat_buf = [Buf("stat%d" % i) for i in range(4)]
    qstat_buf = [Buf("qstat%d" % i) for i in range(3)]

    SMAX = max(seqs)
    mixT, _ = carve(0, [128, 8, SMAX], BF16)
    assert 8 * SMAX * 2 <= 64 * K
    mix_buf = [[Buf("mix%d_%d" % (c, g)) for g in range(SMAX // 512)] for c in range(8)]
    o = 64 * K
    att_slots = []
    for i in range(2):
        q0, o = carve(o, [128, SMAX], BF16)
        q1, o = carve(o, [128, SMAX], BF16)
        k, o = carve(o, [128, SMAX], BF16)
        v, o = carve(o, [128, SMAX // 128, 192], BF16)
        att_slots.append(((q0, q1), k, v, Buf("att%d" % i)))
    att_fill_buf = [Buf("attfill%d" % i) for i in range(2)]
    amask, o = carve(o, [128, 20, 512], BF16)
    PT = []
    for i in range(3):
        v, o = carve(o, [128, 1024], BF16)
        PT.append((v, Buf("PT%d" % i)))
    PTh = [[Buf("PTh%d_%d" % (i, g)) for g in range(2)] for i in range(3)]
    Osb, rz = [], []
    for i in range(4):
        v, o = carve(o, [128, 512], F32)
        Osb.append((v, Buf("Osb%d" % i)))
        v, o = carve(o, [128, 512], F32)
        rz.append((v, Buf("rz%d" % i)))
    onb, sqa, rsa, rsa2 = [], [], [], []
    for i in range(2):
        v, o = carve(o, [128, 512], F32)
        rsa2.append((v, Buf("rsa2_%d" % i)))
        v, o = carve(o, [128, 512], F32)
        onb.append((v, Buf("on%d" % i)))
        v, o = carve(o, [128, 512], BF16)
        sqa.append((v, Buf("sqa%d" % i)))
        v, o = carve(o, [128, 512], F32)
        rsa.append((v, Buf("rsa%d" % i)))
    assert o <= ARENA_KB * K, o

    o = 0
    stg32, stg16 = [], []
    for i in range(2):
        v, o = carve(o, [128, 11 * 512], F32)
        stg32.append((v, Buf("stg32_%d" % i)))
        v, o = carve(o, [128, 11 * 512], BF16)
        stg16.append((v, Buf("stg16_%d" % i)))

    def dma(out, in_, reads, writes, name):
        return S.add("sp", (lambda e: [e.dma_start(out=out, in_=in_)]), reads=reads, writes=writes, dma=name)

    dma(ident, ident_d, [], [Bconst], "c0")
    dma(gcols, gcols_d, [], [Bconst], "c0")
    dma(gbc, gbc_d, [], [Bconst], "c0")
    S.add("dve", lambda e: e.memset(ones, 1.0), writes=[Bconst])
    S.add("dve", lambda e: e.memset(zfill, 0.0), writes=[Bconst])
    S.add("dve", lambda e: e.memset(bones, 0.0), writes=[Bconst])
    S.add("dve", lambda e: e.memset(bones[0:64, 0:64], 1.0), writes=[Bconst])
    S.add("dve", lambda e: e.memset(bones[64:128, 64:128], 1.0), writes=[Bconst])
    lp = lamp.rearrange("p (a d) -> p a d", d=64)
    S.add("dve", lambda e: e.tensor_tensor(out=lp, in0=lamv[:, 0:4:2, :], in1=lamv[:, 1:4:2, :], op=ALU.mult),
          reads=[Bconst], writes=[Bsmall])
    S.add("dve", lambda e: e.tensor_reduce(out=small[:, 2:4], in_=lp, axis=AX.X, op=ALU.add),
          reads=[Bsmall], writes=[Bsmall])
    S.add("act", lambda e: e.activation(out=small[:, 4:6], in_=small[:, 2:4], func=AF.Exp),
          reads=[Bsmall], writes=[Bsmall])
    S.add("dve", lambda e: e.scalar_tensor_tensor(out=nlam, in0=small[:, 5:6], scalar=-LAMBDA_INIT, in1=small[:, 4:5],
                                                  op0=ALU.add, op1=ALU.subtract), reads=[Bsmall], writes=[Bsmall])
    S.add("dve", lambda e: e.tensor_scalar(out=gbs, in0=gcols[:, 28:29], scalar1=1.0 - LAMBDA_INIT, scalar2=None,
                                           op0=ALU.mult), reads=[Bconst, Bsmall], writes=[Bsmall])

    cv = [0]
    wbuf = {}

    def convert(src_view, nk, dst, key):
        i = cv[0] % 4
        cv[0] += 1
        b = Buf("w_%s_%d" % key)
        wbuf[key] = b
        S.add("pool", (lambda e: [e.dma_start(out=dst.rearrange("p (k c) -> p k c", c=512), in_=src_view)]),
              writes=[b], dma="cv%d" % i)

    def conv_rows8(w, blocks, dst, name):
        wv = w.rearrange("(k p) c -> p k c", p=128)
        for b in blocks:
            convert(wv[:, :, b * 512:(b + 1) * 512], 8, dst[b], (name, b))

    def conv_wd(w, dst, name):
        wv = w.rearrange("(f p) c -> p f c", p=128)
        for n in range(2):
            for fh in range(2):
                convert(wv[:, fh * 11:(fh + 1) * 11, n * 512:(n + 1) * 512], 11, dst[n * 2 + fh], (name, n * 2 + fh))

    gu_order = []
    for f in range(NF):
        for b in (f // 4, (NF + f) // 4):
            if b not in gu_order:
                gu_order.append(b)

    def convert_group(g):
        if g == 0:
            conv_rows8(wgu_d[0], gu_order, wgus[0], "gu0")
            conv_wd(wd_d[0], wds[0], "wd0")
            conv_rows8(win_d, range(6), wins, "win")
        else:
            conv_rows8(wout_d, range(2), wouts, "wout")
            conv_rows8(wgu_d[1], gu_order, wgus[1], "gu1")
            conv_wd(wd_d[1], wds[1], "wd1")

    convert_group(0)

    def wblk(ws, b):
        return ws[b].rearrange("p (k c) -> p k c", c=512)

    def plan_ffn_weights(which):
        order = []
        seen = set()
        for f in range(NF):
            for key in (("g", f // 4), ("u", (NF + f) // 4)):
                if key not in seen:
                    seen.add(key)
                    order.append(key)
        wring.plan([(wblk(wgus[which], b), wbuf[("gu%d" % which, b)]) for (_, b) in order])
        dring.plan([(wds[which][i].rearrange("p (f c) -> p f c", c=512), wbuf[("wd%d" % which, i)]) for i in range(4)])
        wring.prefetch(wring.used + 2)
        dring.prefetch(dring.used + 1)
        return order

    sgc = [0]

    def norm_to_hT(xs, gbase):
        xv = xt[xs]
        st = stat_buf[0]
        for j in range(4):
            S.add("act", (lambda e, j=j: e.activation(out=hb[:, j, :], in_=xv[:, j, :], func=AF.Square,
                                                      accum_out=stat[:, j:j + 1])),
                  reads=[xt_buf[xs][j]], writes=[hb_buf[j], st])
        S.add("dve", lambda e: e.tensor_scalar(out=stat[:, 4:8], in0=stat[:, 0:4], scalar1=1.0 / DM, scalar2=EPS,
                                               op0=ALU.mult, op1=ALU.add), reads=[st], writes=[st])
        S.add("act", lambda e: e.activation(out=stat[:, 4:8], in_=stat[:, 4:8], func=AF.Sqrt), reads=[st], writes=[st])
        S.add("dve", lambda e: e.reciprocal(out=stat[:, 8:12], in_=stat[:, 4:8]), reads=[st], writes=[st])
        for j in range(4):
            S.add("act", (lambda e, j=j: e.mul(out=hb[:, j, :], in_=xv[:, j, :], mul=stat[:, 8 + j:9 + j])),
                  reads=[xt_buf[xs][j], st], writes=[hb_buf[j]])
        fb = next_bank()

        def filler(e, fb=fb):
            ins = None
            for _ in range(NFILL):
                ins = e.matmul(bank_ap[fb], lhsT=zfill[:, 0:128], rhs=zfill, start=True, stop=True)
            return ins
        S.add("pe", filler, reads=[Bconst], writes=[bank_buf[fb]])
        for kp in range(4):
            bi = next_bank()
            pb = bank_ap[bi].bitcast(BF16)

            def tr(e, kp=kp, pb=pb):
                ins = None
                for kk in range(2):
                    k = kp * 2 + kk
                    for j in range(4):
                        ins = e.transpose(out=pb[:, kk * 512 + j * 128: kk * 512 + (j + 1) * 128],
                                          in_=hb[:, j, k * 128:(k + 1) * 128], identity=ident)
                return ins
            S.add("pe", tr, reads=hb_buf + [Bconst], writes=[bank_buf[bi]])
            for kk in range(2):
                k = kp * 2 + kk
                S.add("dve", (lambda e, k=k, kk=kk, pb=pb: e.tensor_scalar(
                    out=hT[:, k, :], in0=pb[:, kk * 512:(kk + 1) * 512], scalar1=gcols[:, gbase + k:gbase + k + 1],
                    scalar2=None, op0=ALU.mult)), reads=[bank_buf[bi], Bconst], writes=[hT_buf])
        fb2 = next_bank()

        def filler2(e, fb2=fb2):
            ins = None
            for _ in range(NFILL2):
                ins = e.matmul(bank_ap[fb2], lhsT=zfill[:, 0:128], rhs=zfill, start=True, stop=True)
            return ins
        S.add("pe", filler2, reads=[Bconst], writes=[bank_buf[fb2]])

    planned = [False]

    def ffn(xs, gbase, which):
        if planned[0]:
            planned[0] = False
        else:
            plan_ffn_weights(which)
        norm_to_hT(xs, gbase)
        slot_of = {}
        for f in range(NF):
            for key in (("g", f // 4), ("u", (NF + f) // 4)):
                if key not in slot_of:
                    slot_of[key] = wring.next(ahead=1)
            (wg, wgb, wgi) = slot_of[("g", f // 4)]
            (wu, wub, wui) = slot_of[("u", (NF + f) // 4)]
            wring.check(wgi)
            wring.check(wui)
            og = (f % 4) * 128
            ou = ((NF + f) % 4) * 128
            bg = next_bank()
            bu = next_bank()

            def mmg(e, w=wg, o_=og, b=bg):
                ins = None
                for k in range(8):
                    ins = e.matmul(bank_ap[b], lhsT=w[:, k, o_:o_ + 128], rhs=hT[:, k, :], start=(k == 0), stop=(k == 7))
                return ins
            def mmgu(e, wg=wg, og=og, bg=bg, wu=wu, ou=ou, bu=bu, mmg=mmg):
                mmg(e, wg, og, bg)
                return mmg(e, wu, ou, bu)
            S.add("pe", mmgu, reads=[wgb, wub, hT_buf], writes=[bank_buf[bg], bank_buf[bu]])
            sgv, sgb = sg[sgc[0] % 2]
            sgc[0] += 1
            S.add("act", (lambda e, sgv=sgv, b=bg: e.activation(out=sgv, in_=bank_ap[b], func=AF.Silu)),
                  reads=[bank_buf[bg]], writes=[sgb])
            S.add("dve", (lambda e, sgv=sgv, b=bu, f=f: e.tensor_tensor(out=aT[:, f, :], in0=sgv, in1=bank_ap[b],
                                                                        op=ALU.mult)),
                  reads=[sgb, bank_buf[bu]], writes=[aT_buf[f]])
        for n in range(2):
            banks = [next_bank() for _ in range(4)]
            for fh in range(2):
                wv, wb, _ = dring.next(ahead=1)
                for j in range(4):
                    def mmd(e, wv=wv, j=j, fh=fh, b=banks[j]):
                        ins = None
                        for i in range(11):
                            ins = e.matmul(bank_ap[b], lhsT=aT[:, fh * 11 + i, j * 128:(j + 1) * 128], rhs=wv[:, i, :],
                                           start=(fh == 0 and i == 0), stop=(fh == 1 and i == 10))
                        return ins
                    S.add("pe", mmd, reads=[wb] + aT_buf[fh * 11:(fh + 1) * 11], writes=[bank_buf[banks[j]]])
            for j in range(4):
                xsl = xt[xs][:, j, n * 512:(n + 1) * 512]
                S.add("dve", (lambda e, xsl=xsl, b=banks[j]: e.scalar_tensor_tensor(
                    out=xsl, in0=bank_ap[b], scalar=0.5, in1=xsl, op0=ALU.mult, op1=ALU.add)),
                    reads=[bank_buf[banks[j]], xt_buf[xs][j]], writes=[xt_buf[xs][j]])

    def load_x(src, r0, xs):
        dma(xt[xs], src[r0:r0 + T, :].rearrange("(j p) c -> p j c", p=128), [], xt_buf[xs], "xl%d" % xs)

    conv_done = [False]

    def phase1(tok0, Sq):
        nt = Sq // T
        load_x(x_d, tok0, 0)
        qc = [0]
        qkc = [0]
        for t in range(nt):
            xs = t % 2
            r0 = tok0 + t * T
            if not conv_done[0] and (t == min(3, nt - 1)):
                conv_done[0] = True
                convert_group(1)
            cv_, cb_ = cst[t % 2]
            dma(cv_, cs_d[r0:r0 + T, :].rearrange("(j p) c -> p j c", p=128), [], [cb_], "csl%d" % (t % 2))
            if t + 1 < nt:
                load_x(x_d, r0 + T, 1 - xs)
            ffn(xs, 0, 0)
            wring.plan([(wblk(wins, b), wbuf[("win", b)]) for b in range(6)])
            wring.prefetch(wring.used + 2)
            dma(x1s[r0:r0 + T, :].rearrange("(j p) c -> p j c", p=128), xt[xs], xt_buf[xs], [], "x1st%d" % xs)
            norm_to_hT(xs, 8)
            for j in range(4):
                S.add("pool", (lambda e, j=j, cv_=cv_: e.tensor_tensor(
                    out=gcs[:, j, :, :].rearrange("p t (h d) -> p t h d", d=64),
                    in0=cv_[:, j, :].rearrange("p (h d) -> p h d", d=64).unsqueeze(1).to_broadcast([128, 4, 2, 64]),
                    in1=g44, op=ALU.mult)),
                    reads=[cb_, Bconst], writes=[gcs_buf])
            chains = []
            tick = [0]

            def advance():
                tk = tick[0]
                for born, ch in chains:
                    age = tk - born
                    if age in ch:
                        ch[age]()
                while chains and tk - chains[0][0] >= 5:
                    chains.pop(0)
                tick[0] += 1

            for n in range(6):
                wv, wb, _ = wring.next(ahead=2)
                for j in range(4):
                    bi = next_bank()

                    def mmp(e, wv=wv, j=j, bi=bi):
                        ins = None
                        for k in range(8):
                            ins = e.matmul(bank_ap[bi], lhsT=hT[:, k, j * 128:(j + 1) * 128], rhs=wv[:, k, :],
                                           start=(k == 0), stop=(k == 7))
                        return ins
                    S.add("pe", mmp, reads=[wb, hT_buf], writes=[bank_buf[bi]])
                    if n in (2, 5):
                        c0 = 0 if n == 2 else 512
                        S.add("act", (lambda e, j=j, bi=bi, c0=c0: e.copy(out=Vst[:, j, c0:c0 + 512], in_=bank_ap[bi])),
                              reads=[bank_buf[bi]], writes=[Vst_buf])
                        advance()
                        continue
                    ty = {0: 0, 1: 1, 3: 2, 4: 3}[n]
                    kq = qc[0]
                    qc[0] += 1
                    sqv, sqB = sqb[kq % 2]
                    xnv, xnB = xnb[kq % 3]
                    t1v, t1B = t1b[kq % 3]
                    t2v, t2B = t2b[kq % 2]
                    qkv, qkB = qkb[kq % len(qkb)]
                    st = qstat_buf[kq % 3]
                    so = 16 + (kq % 3) * 24
                    gc_ = gcs[:, j, ty, 0:64]
                    gs_ = gcs[:, j, ty, 64:128]
                    x3 = xnv.rearrange("p (g d) -> p g d", d=64)
                    t3 = t2v.rearrange("p (g d) -> p g d", d=64)

                    def st0(bi=bi, sqv=sqv, sqB=sqB, st=st, so=so):
                        S.add("act", (lambda e: e.activation(out=sqv, in_=bank_ap[bi], func=AF.Square)),
                              reads=[bank_buf[bi]], writes=[sqB])
                        S.add("dve", (lambda e: e.tensor_reduce(out=stat[:, so:so + 8], in_=sqv.rearrange("p (g d) -> p g d", d=64),
                                                                axis=AX.X, op=ALU.add)), reads=[sqB], writes=[st])
                        S.add("dve", (lambda e: e.tensor_scalar(out=stat[:, so + 8:so + 16], in0=stat[:, so:so + 8],
                                                                scalar1=1.0 / 64, scalar2=EPS, op0=ALU.mult, op1=ALU.add)),
                              reads=[st], writes=[st])

                    def st1(st=st, so=so):
                        S.add("act", (lambda e: e.activation(out=stat[:, so + 8:so + 16], in_=stat[:, so + 8:so + 16],
                                                             func=AF.Sqrt)), reads=[st], writes=[st])

                    def st2(bi=bi, st=st, so=so, xnv=xnv, xnB=xnB, t1v=t1v, t1B=t1B, gc_=gc_):
                        S.add("dve", (lambda e: e.reciprocal(out=stat[:, so + 16:so + 24], in_=stat[:, so + 8:so + 16])),
                              reads=[st], writes=[st])
                        S.add("dve", (lambda e: e.tensor_tensor(
                            out=xnv.rearrange("p (g d) -> p g d", d=64), in0=bank_ap[bi].rearrange("p (g d) -> p g d", d=64),
                            in1=stat[:, so + 16:so + 24].unsqueeze(2).to_broadcast([128, 8, 64]), op=ALU.mult)),
                            reads=[bank_buf[bi], st], writes=[xnB])
                        S.add("dve", (lambda e: e.tensor_tensor(
                            out=t1v.rearrange("p (g d) -> p g d", d=64), in0=xnv.rearrange("p (g d) -> p g d", d=64),
                            in1=gc_.unsqueeze(1).to_broadcast([128, 8, 64]), op=ALU.mult)),
                            reads=[xnB, gcs_buf], writes=[t1B])

                    def st3(x3=x3, t3=t3, gs_=gs_, xnB=xnB, t2B=t2B):
                        S.add("pool", (lambda e: e.tensor_tensor(out=t3[:, :, 0:32], in0=x3[:, :, 32:64],
                                                                 in1=gs_[:, 0:32].unsqueeze(1).to_broadcast([128, 8, 32]),
                                                                 op=ALU.mult)), reads=[xnB, gcs_buf], writes=[t2B])
                        S.add("pool", (lambda e: e.tensor_tensor(out=t3[:, :, 32:64], in0=x3[:, :, 0:32],
                                                                 in1=gs_[:, 32:64].unsqueeze(1).to_broadcast([128, 8, 32]),
                                                                 op=ALU.mult)), reads=[xnB, gcs_buf], writes=[t2B])

                    def st4(qkv=qkv, qkB=qkB, t1v=t1v, t1B=t1B, t2v=t2v, t2B=t2B):
                        S.add("dve", (lambda e: e.tensor_tensor(out=qkv, in0=t1v, in1=t2v, op=ALU.add)),
                              reads=[t1B, t2B], writes=[qkB])

                    def st5(qkv=qkv, qkB=qkB, ty=ty, j=j):
                        b2 = next_bank()
                        pb = bank_ap[b2].bitcast(BF16)

                        def trq(e):
                            ins = None
                            for cc in range(4):
                                ins = e.transpose(out=pb[:, cc * 128:(cc + 1) * 128], in_=qkv[:, cc * 128:(cc + 1) * 128],
                                                  identity=ident)
                            return ins
                        S.add("pe", trq, reads=[qkB, Bconst], writes=[bank_buf[b2]])
                        S.add("act", (lambda e: e.copy(
                            out=QKst[:, ty * 4:(ty + 1) * 4, j * 128:(j + 1) * 128],
                            in_=pb[:, 0:512].rearrange("p (c t) -> p c t", t=128))),
                            reads=[bank_buf[b2]], writes=[QKst_buf])
                    chains.append((tick[0], {0: st0, 1: st1, 2: st2, 3: st3, 4: st4, 5: st5}))
                    advance()
            if t + 1 < nt:
                plan_ffn_weights(0)
                planned[0] = True
            while chains:
                advance()
            dma(qts[:, :, r0:r0 + T].rearrange("c p t -> p c t"), QKst, [QKst_buf], [], "qst")
            dma(vs[r0:r0 + T, :].rearrange("(j p) c -> p j c", p=128), Vst, [Vst_buf], [], "vst")

    def attention(tok0, Sq):
        NKB = Sq // 128
        NQG = Sq // 512
        ptc = [0]
        sc = [0]
        ozc = [0]
        pc = [0]
        for sl in range(2):
            (q0, q1), k, v, b = att_slots[sl]
            zb = att_fill_buf[sl]
            S.add("dve", (lambda e, q0=q0: e.memset(q0[64:128, 0:Sq], 0.0)), writes=[zb])
            S.add("dve", (lambda e, q1=q1: e.memset(q1[0:64, 0:Sq], 0.0)), writes=[zb])
            S.add("dve", (lambda e, v=v: e.memset(v[:, 0:NKB, 64:128], 1.0)), writes=[zb])

        def load_chunk(c, slot):
            (q0, q1), k, v, b = att_slots[slot]
            if c < 4:
                qi, ki, vc = c, 4 + c, c * 128
            else:
                qi, ki, vc = 8 + (c - 4), 12 + (c - 4), 512 + (c - 4) * 128
            vsrc = vs[tok0:tok0 + Sq, :].rearrange("(kb p) c -> p kb c", p=128)
            if c < 4:
                vd = [(v[:, 0:NKB, 0:64], vsrc[:, :, vc:vc + 64]), (v[:, 0:NKB, 128:192], vsrc[:, :, vc + 64:vc + 128])]
            else:
                vd = [(v[:, 0:NKB, 0:128], vsrc[:, :, vc:vc + 128])]
            S.add("sp", (lambda e: [e.dma_start(out=q0[0:64, 0:Sq], in_=qts[qi, 0:64, tok0:tok0 + Sq]),
                                    e.dma_start(out=q1[64:128, 0:Sq], in_=qts[qi, 64:128, tok0:tok0 + Sq]),
                                    e.dma_start(out=k[:, 0:Sq], in_=qts[ki, :, tok0:tok0 + Sq])]
                         + [e.dma_start(out=o_, in_=i_) for (o_, i_) in vd]),
                  writes=[b] + ([att_fill_buf[slot]] if c >= 4 else []), dma="att%d" % slot, ndma=3 + len(vd))

        def make_post(c, qg, isA, streams):
            pp = (c * NQG + qg) % 2
            onv, onB = onb[pp]
            sqv, sqB = sqa[pp]
            rsv, rsB = rsa[pp]
            zev, zeB = rsa2[pp]
            (o0, o0B, z0, z0B), (o1, o1B, z1, z1B) = streams
            if isA:
                lhs_ss, inv_n, gcol = bones, 1.0 / 64, gcols[:, 24 + c:25 + c]
            else:
                lhs_ss, inv_n, gcol = ones, 1.0 / 128, gbs
            mo = mixT[:, c, qg * 512:(qg + 1) * 512]

            def s1a():
                if isA:
                    S.add("dve", (lambda e: e.scalar_tensor_tensor(out=zev[0:64, :], in0=z0[0:64, :], scalar=EPS,
                                                                   in1=z0[0:64, :], op0=ALU.mult, op1=ALU.mult)),
                          reads=[z0B], writes=[zeB])
                    S.add("dve", (lambda e: e.scalar_tensor_tensor(out=zev[64:128, :], in0=z1[64:128, :], scalar=EPS,
                                                                   in1=z1[64:128, :], op0=ALU.mult, op1=ALU.mult)),
                          reads=[z1B], writes=[zeB])
                else:
                    S.add("dve", (lambda e: e.reciprocal(out=z1, in_=z1)), reads=[z1B], writes=[z1B])
                    S.add("dve", (lambda e: e.tensor_tensor(out=z1, in0=z1, in1=z0, op=ALU.mult)), reads=[z1B, z0B], writes=[z1B])
                    S.add("dve", (lambda e: e.tensor_tensor(out=o1, in0=o1, in1=z1, op=ALU.mult)), reads=[o1B, z1B], writes=[o1B])
                    S.add("dve", (lambda e: e.scalar_tensor_tensor(out=onv, in0=o1, scalar=nlam, in1=o0, op0=ALU.mult,
                                                                   op1=ALU.add)), reads=[o0B, o1B, Bsmall], writes=[onB])
                    S.add("dve", (lambda e: e.scalar_tensor_tensor(out=zev, in0=z0, scalar=EPS, in1=z0, op0=ALU.mult,
                                                                   op1=ALU.mult)), reads=[z0B], writes=[zeB])

            def s1b():
                if isA:
                    S.add("dve", (lambda e: e.tensor_tensor(out=sqv[0:64, :], in0=o0[0:64, :], in1=o0[0:64, :], op=ALU.mult)),
                          reads=[o0B], writes=[sqB])
                    S.add("dve", (lambda e: e.tensor_tensor(out=sqv[64:128, :], in0=o1[64:128, :], in1=o1[64:128, :],
                                                            op=ALU.mult)), reads=[o1B], writes=[sqB])
                else:
                    S.add("act", (lambda e: e.activation(out=sqv, in_=onv, func=AF.Square)), reads=[onB], writes=[sqB])

            def s2a():
                sp_ = 2 * (sc[0] % (3 if isA else 2))
                sc[0] += 1
                S.add("pe", (lambda e: e.matmul(bank_ap[sp_], lhsT=lhs_ss, rhs=sqv, start=True, stop=True)),
                      reads=[sqB, Bconst], writes=[bank_buf[sp_]])
                S.add("dve", (lambda e: e.scalar_tensor_tensor(out=rsv, in0=bank_ap[sp_], scalar=inv_n, in1=zev, op0=ALU.mult,
                                                               op1=ALU.add)), reads=[bank_buf[sp_], zeB], writes=[rsB])

            def s2b():
                S.add("act", (lambda e: e.activation(out=rsv, in_=rsv, func=AF.Ln)), reads=[rsB], writes=[rsB])
                S.add("act", (lambda e: e.activation(out=rsv, in_=rsv, func=AF.Exp, scale=-0.5)), reads=[rsB], writes=[rsB])

            def s2c():
                if isA:
                    S.add("dve", (lambda e: e.scalar_tensor_tensor(out=mo[0:64, :], in0=o0[0:64, :], scalar=gcol[0:64, :],
                                                                   in1=rsv[0:64, :], op0=ALU.mult, op1=ALU.mult)),
                          reads=[o0B, rsB, Bconst], writes=[mix_buf[c][qg]])
                    S.add("dve", (lambda e: e.scalar_tensor_tensor(out=mo[64:128, :], in0=o1[64:128, :],
                                                                   scalar=gcol[64:128, :], in1=rsv[64:128, :],
                                                                   op0=ALU.mult, op1=ALU.mult)),
                          reads=[o1B, rsB, Bconst], writes=[mix_buf[c][qg]])
                else:
                    S.add("dve", (lambda e: e.scalar_tensor_tensor(out=mo, in0=onv, scalar=gcol, in1=rsv, op0=ALU.mult,
                                                                   op1=ALU.mult)),
                          reads=[onB, rsB, Bconst, Bsmall], writes=[mix_buf[c][qg]])
            if isA:
                return [(1, s1b), (3, s1a), (5, s2a), (6, s2b), (7, s2c)]
            return [(0, s1a), (6, s1b), (8, s2a), (10, s2b), (12, s2c)]

        def make_steps(chunks):
            steps = []
            for c in chunks:
                isA = c < 4
                for qg in range(NQG):
                    for e_ in range(2):
                        if isA:
                            kb_lo, kb_hi = max(0, 4 * qg - 8), min(NKB - 1, 4 * qg + 11)
                        else:
                            kb_lo, kb_hi = 0, NKB - 1
                        kbs = list(range(kb_lo, kb_hi + 1))
                        grps = [kbs[i:i + 2] for i in range(0, len(kbs), 2)]
                        for gi, grp in enumerate(grps):
                            steps.append(dict(c=c, qg=qg, e=e_, grp=grp, first=(gi == 0), last=(gi == len(grps) - 1), isA=isA))
            return steps

        state = {"streams": [], "oz": None}
        deferred = []

        def front(st):
            c, qg, e_, grp = st["c"], st["qg"], st["e"], st["grp"]
            Qs, KT, V, ab = att_slots[c % 2]
            Qe = Qs[e_]
            ng = len(grp)
            sp_ = 2 * (sc[0] % (3 if st["isA"] else 2))
            sc[0] += 1
            sb = [bank_buf[sp_ + g] for g in range(ng)]

            def qk(e):
                ins = None
                for g, kb in enumerate(grp):
                    ins = e.matmul(bank_ap[sp_ + g], lhsT=KT[:, kb * 128:(kb + 1) * 128],
                                   rhs=Qe[:, qg * 512:(qg + 1) * 512], start=True, stop=True)
                return ins
            S.add("pe", qk, reads=[ab, att_fill_buf[c % 2]], writes=sb)
            pslot = ptc[0] % 3
            pv_, pB = PT[pslot]
            ptc[0] += 1
            if st["isA"]:
                S.add("act", (lambda e: e.activation(out=pv_[:, 0:ng * 512], in_=ps_all[:, sp_ * 512:(sp_ + ng) * 512],
                                                     func=AF.Exp, scale=0.125)), reads=sb, writes=[pB])
            else:
                for g in range(ng):
                    S.add("act", (lambda e, g=g: e.activation(out=pv_[:, g * 512:(g + 1) * 512], in_=bank_ap[sp_ + g],
                                                              func=AF.Exp, scale=0.125)),
                          reads=[sb[g]], writes=([pB] if g == 0 else []) + [PTh[pslot][g]])
            st["pth"] = PTh[pslot]
            if st["isA"]:
                rel = grp[0] - 4 * qg + 8
                S.add("dve", (lambda e: e.tensor_tensor(out=pv_[:, 0:ng * 512], in0=pv_[:, 0:ng * 512],
                                                        in1=amask[:, rel:rel + ng, :].rearrange("p r q -> p (r q)"),
                                                        op=ALU.mult)), reads=[pB, Bconst], writes=[pB])
            st["pt"] = (pv_, pB)

        def back(st, idx):
            c, qg, e_, grp = st["c"], st["qg"], st["e"], st["grp"]
            if st["first"] and e_ == 0 and qg == 0 and c < 7:
                load_chunk(c + 1, (c + 1) % 2)
            Qs, KT, V, ab = att_slots[c % 2]
            pv_, pB = st["pt"]
            isA_ = st["isA"]
            if st["first"]:
                if isA_:
                    state["oz"] = 6 + (ozc[0] % 2)
                else:
                    state["oz"] = 4 + 2 * (ozc[0] % 2)
                ozc[0] += 1
            Ob = state["oz"]
            Zb = Ob + 1
            fst, lst = st["first"], st["last"]
            ng = len(grp)

            if isA_:
                def pvm(e):
                    ins = None
                    for g, kb in enumerate(grp):
                        ins = e.matmul(bank_ap[Ob], lhsT=V[:, kb, e_ * 64:e_ * 64 + 128], rhs=pv_[:, g * 512:(g + 1) * 512],
                                       start=(fst and g == 0), stop=(lst and g == ng - 1))
                    return ins
                S.add("pe", pvm, reads=[pB, ab, att_fill_buf[c % 2]], writes=[bank_buf[Ob]])
            else:
                for g, kb in enumerate(grp):
                    def pvm(e, g=g, kb=kb):
                        a_ = fst and g == 0
                        z_ = lst and g == ng - 1
                        e.matmul(bank_ap[Ob], lhsT=V[:, kb, 0:128], rhs=pv_[:, g * 512:(g + 1) * 512], start=a_, stop=z_)
                        return e.matmul(bank_ap[Zb], lhsT=ones, rhs=pv_[:, g * 512:(g + 1) * 512], start=a_, stop=z_)
                    S.add("pe", pvm, reads=[st["pth"][g], ab, Bconst], writes=[bank_buf[Ob], bank_buf[Zb]])
            if lst:
                pi = pc[0] % 4
                pc[0] += 1
                ov, oB = Osb[pi]
                rv, rB = rz[pi]
                if isA_:
                    S.add("dve", (lambda e: e.tensor_copy(out=ov, in_=bank_ap[Ob])), reads=[bank_buf[Ob]], writes=[oB])
                else:
                    S.add("act", (lambda e: e.copy(out=ov, in_=bank_ap[Ob])), reads=[bank_buf[Ob]], writes=[oB])
                if isA_:
                    if e_ == 0:
                        S.add("sp", (lambda e: [e.dma_start(out=rv[0:64, :], in_=ov[64:128, :])]), reads=[oB], writes=[rB],
                              dma="zmv%d" % (pi % 2))
                    else:
                        S.add("sp", (lambda e: [e.dma_start(out=rv[64:128, :], in_=ov[0:64, :])]), reads=[oB], writes=[rB],
                              dma="zmv%d" % (pi % 2))
                else:
                    S.add("dve", (lambda e: e.tensor_copy(out=rv, in_=bank_ap[Zb])), reads=[bank_buf[Zb]], writes=[rB])
                state["streams"].append((ov, oB, rv, rB))
                if e_ == 1:
                    strs = state["streams"]
                    state["streams"] = []
                    cap = 3 * ((min(NKB, 12) + 1) // 2) - 1
                    for dly, fn in make_post(c, qg, st["isA"], strs):
                        deferred.append((idx + min(dly, cap), fn))
                    deferred.sort(key=lambda t: t[0])

        load_chunk(0, 0)
        dma(amask, amask_d.rearrange("p (r q) -> p r q", q=512), [], [Bconst], "c0")
        for chunks, LAG in ((range(0, 4), 2), (range(4, 8), 1)):
            steps = make_steps(chunks)
            n = len(steps)
            for i in range(n + LAG):
                if i < n:
                    front(steps[i])
                if i - LAG >= 0:
                    back(steps[i - LAG], i - LAG)
                while deferred and deferred[0][0] <= i - LAG:
                    deferred.pop(0)[1]()
            while deferred:
                deferred.pop(0)[1]()
        if debug:
            for c in range(8):
                dma(mixd[c, :, tok0:tok0 + Sq], mixT[:, c, 0:Sq], [mix_buf[c][g] for g in range(NQG)], [], "dbg")

    def phase3(tok0, Sq):
        nt = Sq // T
        load_x(x1s, tok0, 0)
        for t in range(nt):
            xs = t % 2
            r0 = tok0 + t * T
            if t + 1 < nt:
                load_x(x1s, r0 + T, 1 - xs)
            if t == 0:
                wring.plan([(wblk(wouts, b), wbuf[("wout", b)]) for b in range(2)])
                wring.prefetch(wring.used + 2)
                plan_ffn_weights(1)
                planned[0] = True
            for n in range(2):
                wv, wb, _ = wring.next(ahead=2)
                for j in range(4):
                    bi = next_bank()
                    tk = t * T + j * 128

                    def mmo(e, wv=wv, tk=tk, bi=bi):
                        ins = None
                        for k in range(8):
                            ins = e.matmul(bank_ap[bi], lhsT=mixT[:, k, tk:tk + 128], rhs=wv[:, k, :],
                                           start=(k == 0), stop=(k == 7))
                        return ins
                    S.add("pe", mmo, reads=[wb] + [mix_buf[k][t] for k in range(8)], writes=[bank_buf[bi]])
                    xsl = xt[xs][:, j, n * 512:(n + 1) * 512]
                    S.add("dve", (lambda e, xsl=xsl, bi=bi: e.tensor_tensor(out=xsl, in0=bank_ap[bi], in1=xsl, op=ALU.add)),
                          reads=[bank_buf[bi], xt_buf[xs][j]], writes=[xt_buf[xs][j]])
            ffn(xs, 16, 1)
            if t + 1 < nt:
                wring.plan([(wblk(wouts, b), wbuf[("wout", b)]) for b in range(2)])
                wring.prefetch(wring.used + 2)
                plan_ffn_weights(1)
                planned[0] = True
            st = stat_buf[3]
            for j in range(4):
                S.add("act", (lambda e, j=j, xs=xs: e.activation(out=hb[:, j, :], in_=xt[xs][:, j, :], func=AF.Square,
                                                                 accum_out=stat[:, 12 + j:13 + j])),
                      reads=[xt_buf[xs][j]], writes=[hb_buf[j], st])
            S.add("dve", lambda e: e.tensor_scalar(out=stat[:, 16:20], in0=stat[:, 12:16], scalar1=1.0 / DM, scalar2=EPS,
                                                   op0=ALU.mult, op1=ALU.add), reads=[st], writes=[st])
            S.add("act", lambda e: e.activation(out=stat[:, 16:20], in_=stat[:, 16:20], func=AF.Sqrt),
                  reads=[st], writes=[st])
            S.add("dve", lambda e: e.reciprocal(out=stat[:, 20:24], in_=stat[:, 16:20]), reads=[st], writes=[st])
            for j in range(4):
                S.add("dve", (lambda e, j=j, xs=xs: e.scalar_tensor_tensor(
                    out=xt[xs][:, j, :], in0=xt[xs][:, j, :], scalar=stat[:, 20 + j:21 + j], in1=gfin,
                    op0=ALU.mult, op1=ALU.mult)), reads=[xt_buf[xs][j], st, Bconst], writes=[xt_buf[xs][j]])
            dma(y_d[r0:r0 + T, :].rearrange("(j p) c -> p j c", p=128), xt[xs], xt_buf[xs], [], "yst%d" % xs)

    tok0 = 0
    for Sq in seqs:
        phase1(tok0, Sq)
        S.barrier()
        attention(tok0, Sq)
        S.barrier()
        phase3(tok0, Sq)
        S.barrier()
        tok0 += Sq
    S.emit_all(nc)
    return nc


def _mult(delta):
    d = np.abs(delta)
    w = (d <= 64).astype(np.float32)
    w += ((delta % 4 == 0) & (d <= 256)).astype(np.float32)
    w += ((delta % 16 == 0) & (d <= 1024)).astype(np.float32)
    return w


def host_consts(seqs):
    cs = []
    inv = 1.0 / (10000.0 ** (np.arange(0, 64, 2, dtype=np.float32) / np.float32(64)))
    for Sq in seqs:
        ang = np.arange(Sq, dtype=np.float32)[:, None] * inv[None, :].astype(np.float32)
        ang = np.concatenate([ang, ang], axis=-1)
        c = np.cos(ang).astype(np.float32)
        s = np.sin(ang).astype(np.float32)
        s = np.concatenate([-s[:, :32], s[:, 32:]], axis=-1)
        cs.append(np.concatenate([c, s], axis=-1))
    cs = np.ascontiguousarray(np.concatenate(cs, axis=0), dtype=np.float32)
    kk = np.arange(128)[:, None]
    qq = np.arange(512)[None, :]
    am = np.zeros((128, 20, 512), np.float32)
    for r in range(20):
        rel = r - 8
        delta = 128 * rel + kk - qq
        am[:, r, :] = _mult(delta)
    ident = np.eye(128, dtype=np.float32).astype(ml_dtypes.bfloat16)
    return cs, am.reshape(128, 20 * 512).astype(ml_dtypes.bfloat16), ident


def host_small(inp):
    f = lambda a: np.asarray(a, np.float32)
    gcols = np.zeros((128, 32), np.float32)
    gcols[:, 0:8] = f(inp["g_ffn1"]).reshape(8, 128).T
    gcols[:, 8:16] = f(inp["g_mix"]).reshape(8, 128).T
    gcols[:, 16:24] = f(inp["g_ffn2"]).reshape(8, 128).T
    gcols[:, 24:28] = f(inp["g_a_out"]).reshape(4, 128).T
    gcols[:, 28] = f(inp["g_b_out"]).reshape(128)
    def gsw(a):
        a = f(a).reshape(-1)
        return np.concatenate([a, a[32:], a[:32]])
    row = np.concatenate([f(inp["g_final"]).reshape(-1), gsw(inp["g_a_q"]), gsw(inp["g_a_k"]),
                          gsw(inp["g_b_q"]), gsw(inp["g_b_k"]), f(inp["lam_q1"]).reshape(-1),
                          f(inp["lam_k1"]).reshape(-1), f(inp["lam_q2"]).reshape(-1), f(inp["lam_k2"]).reshape(-1)])
    gbc = np.ascontiguousarray(np.broadcast_to(row[None, :], (128, row.size)))
    return gcols, gbc


_CACHE = {}


def run(xs_per_core, inp, seqs, debug=False, trace=False):
    key = (tuple(seqs), debug)
    if key not in _CACHE:
        _CACHE[key] = build_program(list(seqs), debug=debug)
    nc = _CACHE[key]
    cs, am, ident = host_consts(seqs)
    gcols, gbc = host_small(inp)
    f = lambda a: np.ascontiguousarray(np.asarray(a, np.float32))
    common = {
        "w_gu1": f(inp["w_ffn1_gu"][0]), "w_gu2": f(inp["w_ffn2_gu"][0]),
        "w_d1": f(inp["w_ffn1_down"][0]), "w_d2": f(inp["w_ffn2_down"][0]),
        "w_in": f(inp["w_in"][0]), "w_out": f(inp["w_out"][0]),
        "gcols": gcols, "gbc": gbc, "cs": cs, "ident": ident, "amask": am,
    }
    in_maps = [dict(common, x=f(x)) for x in xs_per_core]
    res = run_bass_kernel_spmd(nc, in_maps, core_ids=list(range(len(in_maps))), trace=trace)
    return res


def kernel(**inputs):
    xp = np.asarray(inputs["x_prompt"], np.float32)
    xsm = np.asarray(inputs["x_sample"], np.float32)
    n = 8
    seqs = (xp.shape[1], xsm.shape[1])
    xs = [np.concatenate([xp[c], xsm[c]], axis=0) for c in range(n)]
    res = run(xs, inputs, seqs)
    ys = [r["y"] for r in res.results]
    yp = np.stack([y[:seqs[0]] for y in ys], axis=0).astype(np.float32)
    ysm = np.stack([y[seqs[0]:] for y in ys], axis=0).astype(np.float32)
    return (yp, ysm)
```

```python
import math
from collections import deque
from contextlib import ExitStack

import numpy as np
import ml_dtypes
import concourse.bass as bass
import concourse.mybir as mybir
from concourse.bass_utils import run_bass_kernel_spmd

F32 = mybir.dt.float32
BF16 = mybir.dt.bfloat16
AF = mybir.ActivationFunctionType
ALU = mybir.AluOpType
AX = mybir.AxisListType

DM = 1024
DFF = 2816
NF = 22
T = 512
EPS = 1e-6
NFILL = 44
NFILL2 = 14
LAMBDA_INIT = 0.8 - 0.6 * math.exp(-0.3 * 0)


class Buf:
    __slots__ = ("name", "w", "r")

    def __init__(self, name):
        self.name = name
        self.w = None
        self.r = {}


class Op:
    __slots__ = ("eng", "emit", "deps", "sig", "seq", "dsem", "dbase", "dend", "idx")


COMPUTE = ("pe", "act", "dve", "pool")
ALLENG = ("pe", "act", "dve", "pool", "sp")


class Sched:
    def __init__(self):
        self.streams = {e: [] for e in ALLENG}
        self.dma_tot = {}
        self.nops = 0
        self.barrier_deps = {}
        self.last_dma = {}

    def add(self, eng, emit, reads=(), writes=(), dma=None, ndma=1):
        op = Op()
        op.eng = eng
        op.emit = emit
        op.sig = False
        op.seq = None
        op.idx = self.nops
        self.nops += 1
        deps = {}
        for b in reads:
            if b.w is not None:
                deps[id(b.w)] = (b.w, True)
        for b in writes:
            if b.w is not None:
                deps[id(b.w)] = (b.w, True)
            for r in b.r.values():
                if id(r) not in deps:
                    deps[id(r)] = (r, False)
        bd = self.barrier_deps.pop(eng, None)
        if bd:
            for d in bd:
                deps[id(d)] = (d, True)
        if dma is not None:
            op.dsem = dma
            op.dbase = self.dma_tot.get(dma, 0)
            op.dend = op.dbase + 16 * ndma
            self.dma_tot[dma] = op.dend
            self.last_dma[dma] = op
        else:
            op.dsem = None
            op.dbase = op.dend = 0
        final = []
        for d, strong in deps.values():
            if d is op:
                continue
            if d.dsem is None and d.eng == eng:
                if eng == "pe":
                    continue
                if not strong:
                    continue
            final.append(d)
            if d.dsem is None:
                d.sig = True
        op.deps = final
        wset = set(id(b) for b in writes)
        for b in writes:
            b.w = op
            b.r = {}
        key = eng if dma is None else ("dma", dma)
        for b in reads:
            if id(b) not in wset:
                b.r[key] = op
        self.streams[eng].append(op)
        return op

    def barrier(self):
        deps = []
        for e in ALLENG:
            for op in reversed(self.streams[e]):
                if op.dsem is None:
                    deps.append(op)
                    break
        deps.extend(self.last_dma.values())
        for d in deps:
            if d.dsem is None:
                d.sig = True
        for e in ALLENG:
            self.barrier_deps[e] = self.barrier_deps.get(e, []) + deps

    def emit_all(self, nc):
        for e in COMPUTE:
            c = 0
            for op in self.streams[e]:
                if op.sig and op.dsem is None:
                    c += 1
                    op.seq = c
        with ExitStack() as es:
            esem = {e: es.enter_context(nc.semaphore("sem_" + e)) for e in COMPUTE}
            dsem = {n: es.enter_context(nc.semaphore("dsem_" + n)) for n in self.dma_tot}
            block = es.enter_context(nc.Block())
            sched = self

            def run(ename, eng):
                waited = {}

                def wait(key, sem, val):
                    if val <= 0:
                        return
                    if waited.get(key, 0) < val:
                        eng.wait_ge(sem, val)
                        waited[key] = val

                for op in sched.streams[ename]:
                    for d in op.deps:
                        if d.dsem is None:
                            wait(d.eng, esem[d.eng], d.seq)
                        else:
                            wait(("d", d.dsem), dsem[d.dsem], d.dend)
                    if op.dsem is not None:
                        wait(("d", op.dsem), dsem[op.dsem], op.dbase)
                        for ins in op.emit(eng):
                            ins.then_inc(dsem[op.dsem], 16)
                    else:
                        ins = op.emit(eng)
                        if op.sig:
                            ins.then_inc(esem[ename], 1)
                if ename == "sp":
                    for n, tot in sched.dma_tot.items():
                        wait(("d", n), dsem[n], tot)

            @block.tensor
            def _(e):
                run("pe", e)

            @block.scalar
            def _(e):
                run("act", e)

            @block.vector
            def _(e):
                run("dve", e)

            @block.gpsimd
            def _(e):
                run("pool", e)

            @block.sync
            def _(e):
                run("sp", e)


class Ring:
    def __init__(self, S, slots, name):
        self.S = S
        self.slots = slots
        self.n = len(slots)
        self.name = name
        self.pending = deque()
        self.issued = 0
        self.used = 0

    def plan(self, srcs):
        self.pending.extend(srcs)

    def prefetch(self, upto):
        while self.issued < upto and self.pending:
            src, sbuf = self.pending.popleft()
            i = self.issued % self.n
            ap, buf = self.slots[i]
            self.S.add("sp", (lambda e, ap=ap, src=src: [e.dma_start(out=ap, in_=src)]),
                       reads=[sbuf], writes=[buf], dma="%s%d" % (self.name, i))
            self.issued += 1

    def next(self, ahead):
        i = self.used
        self.used += 1
        self.prefetch(i + 1 + ahead)
        assert self.issued > i
        ap, buf = self.slots[i % self.n]
        return ap, buf, i

    def check(self, i):
        assert self.issued <= i + self.n, "ring slot overwritten while live"


def build_program(seqs, debug=False):
    NTOK = sum(seqs)
    nc = bass.Bass("TRN2", target_bir_lowering=False)
    S = Sched()

    def din(name, shape, dt=F32):
        return nc.dram_tensor(name, list(shape), dt, kind="ExternalInput").ap()

    x_d = din("x", [NTOK, DM])
    wgu_d = [din("w_gu1", [DM, 2 * DFF]), din("w_gu2", [DM, 2 * DFF])]
    wd_d = [din("w_d1", [DFF, DM]), din("w_d2", [DFF, DM])]
    win_d = din("w_in", [DM, 3072])
    wout_d = din("w_out", [DM, DM])
    gcols_d = din("gcols", [128, 32])
    gbc_d = din("gbc", [128, 1024 + 512 + 256])
    cs_d = din("cs", [NTOK, 128])
    ident_d = din("ident", [128, 128], BF16)
    amask_d = din("amask", [128, 20 * 512], BF16)
    y_d = nc.dram_tensor("y", [NTOK, DM], F32, kind="ExternalOutput").ap()

    ikind = "ExternalOutput" if debug else "Internal"
    x1s = nc.dram_tensor("x1s", [NTOK, DM], F32, kind=ikind).ap()
    qts = nc.dram_tensor("qts", [16, 128, NTOK], BF16, kind=ikind).ap()
    vs = nc.dram_tensor("vs", [NTOK, 1024], BF16, kind=ikind).ap()
    mixd = nc.dram_tensor("mixd", [8, 128, NTOK], BF16, kind=ikind).ap() if debug else None
    wgus = [nc.dram_tensor("wgus%d" % i, [11, 128, 8 * 512], BF16, kind="Internal").ap() for i in range(2)]
    wds = [nc.dram_tensor("wds%d" % i, [4, 128, 11 * 512], BF16, kind="Internal").ap() for i in range(2)]
    wins = nc.dram_tensor("wins", [6, 128, 8 * 512], BF16, kind="Internal").ap()
    wouts = nc.dram_tensor("wouts", [2, 128, 8 * 512], BF16, kind="Internal").ap()

    ARENA_KB = 192
    arena = nc.alloc_sbuf_tensor("arena", [128, ARENA_KB * 256], F32).ap()

    def carve(off, shape, dt):
        n = 1
        for s in shape[1:]:
            n *= s
        nbytes = n * (4 if dt == F32 else 2)
        assert off % 4 == 0 and nbytes % 4 == 0
        assert off + nbytes <= ARENA_KB * 1024, (off, nbytes)
        v = arena[:, off // 4:(off + nbytes) // 4]
        if dt != F32:
            v = v.bitcast(dt)
        if len(shape) == 3:
            v = v.rearrange("p (a b) -> p a b", b=shape[2])
        elif len(shape) == 4:
            v = v.rearrange("p (a b c) -> p a b c", b=shape[2], c=shape[3])
        return v, off + nbytes

    ident = nc.alloc_sbuf_tensor("sb_ident", [128, 128], BF16).ap()
    ones = nc.alloc_sbuf_tensor("sb_ones", [128, 128], BF16).ap()
    bones = nc.alloc_sbuf_tensor("sb_bones", [128, 128], BF16).ap()
    zfill = nc.alloc_sbuf_tensor("sb_zfill", [128, 512], BF16).ap()
    gcols = nc.alloc_sbuf_tensor("sb_gcols", [128, 32], F32).ap()
    gbc = nc.alloc_sbuf_tensor("sb_gbc", [128, 1792], F32).ap()
    small = nc.alloc_sbuf_tensor("sb_small", [128, 64], F32).ap()
    lamp = nc.alloc_sbuf_tensor("sb_lamp", [128, 128], F32).ap()
    Bconst = Buf("const")
    Bsmall = Buf("small")
    gfin = gbc[:, 0:1024]
    g44 = gbc[:, 1024:1536].rearrange("p (t h d) -> p t h d", h=2, d=64)
    lamv = gbc[:, 1536:1792].rearrange("p (t d) -> p t d", d=64)
    nlam = small[:, 0:1]
    gbs = small[:, 1:2]

    ps_all = nc.alloc_psum_tensor("ps", [128, 4096], F32).ap()
    bank_ap = [ps_all[:, i * 512:(i + 1) * 512] for i in range(8)]
    bank_buf = [Buf("bank%d" % i) for i in range(8)]
    bank_ctr = [0]

    def next_bank():
        i = bank_ctr[0] % 8
        bank_ctr[0] += 1
        return i

    K = 1024
    o = 64 * K
    xt = []
    for i in range(2):
        v, o = carve(o, [128, 4, 1024], F32)
        xt.append(v)
    xt_buf = [[Buf("xt%d_%d" % (i, j)) for j in range(4)] for i in range(2)]
    hb, o = carve(o, [128, 4, 1024], BF16)
    hb_buf = [Buf("hb%d" % j) for j in range(4)]
    hT, o = carve(o, [128, 8, 512], BF16)
    hT_buf = Buf("hT")
    aT, o = carve(o, [128, NF, 512], BF16)
    aT_buf = [Buf("aT%d" % f) for f in range(NF)]
    wr_slots = []
    for i in range(4):
        v, o = carve(o, [128, 8, 512], BF16)
        wr_slots.append((v, Buf("wr%d" % i)))
    wd_slots = []
    for i in range(2):
        v, o = carve(o, [128, 11, 512], BF16)
        wd_slots.append((v, Buf("wd%d" % i)))
    sg = []
    for i in range(2):
        v, o = carve(o, [128, 512], BF16)
        sg.append((v, Buf("sg%d" % i)))
    ffn_end = o
    assert ffn_end <= ARENA_KB * K
    wring = Ring(S, wr_slots, "wr")
    dring = Ring(S, wd_slots, "wd")

    o = 0
    QKst, o = carve(o, [128, 16, 512], BF16)
    QKst_buf = Buf("QKst")
    Vst, o = carve(o, [128, 4, 1024], BF16)
    Vst_buf = Buf("Vst")
    cst = []
    for i in range(2):
        v, o = carve(o, [128, 4, 128], F32)
        cst.append((v, Buf("cst%d" % i)))
    gcs, o = carve(o, [128, 4, 4, 128], F32)
    gcs_buf = Buf("gcs")
    sqb, xnb, t1b, t2b, qkb = [], [], [], [], []
    NQS = 3
    for i in range(NQS):
        v, o = carve(o, [128, 512], F32)
        sqb.append((v, Buf("sqb%d" % i)))
        v, o = carve(o, [128, 512], F32)
        xnb.append((v, Buf("xnb%d" % i)))
        v, o = carve(o, [128, 512], F32)
        t1b.append((v, Buf("t1b%d" % i)))
        v, o = carve(o, [128, 512], F32)
        t2b.append((v, Buf("t2b%d" % i)))
        v, o = carve(o, [128, 512], BF16)
        qkb.append((v, Buf("qkb%d" % i)))
    p1_end = o
    assert p1_end <= 64 * K, p1_end
    o = ffn_end
    for i in range(2):
        v, o = carve(o, [128, 512], BF16)
        qkb.append((v, Buf("qkb%d" % (NQS + i))))
    stat = nc.alloc_sbuf_tensor("sb_stat", [128, 96], F32).ap()
    stat_buf = [Buf("stat%d" % i) for i in range(4)]
    qstat_buf = [Buf("qstat%d" % i) for i in range(3)]

    SMAX = max(seqs)
    mixT, _ = carve(0, [128, 8, SMAX], BF16)
    assert 8 * SMAX * 2 <= 64 * K
    mix_buf = [[Buf("mix%d_%d" % (c, g)) for g in range(SMAX // 512)] for c in range(8)]
    o = 64 * K
    att_slots = []
    for i in range(2):
        q0, o = carve(o, [128, SMAX], BF16)
        q1, o = carve(o, [128, SMAX], BF16)
        k, o = carve(o, [128, SMAX], BF16)
        v, o = carve(o, [128, SMAX // 128, 192], BF16)
        att_slots.append(((q0, q1), k, v, Buf("att%d" % i)))
    amask, o = carve(o, [128, 20, 512], BF16)
    PT = []
    for i in range(3):
        v, o = carve(o, [128, 1024], BF16)
        PT.append((v, Buf("PT%d" % i)))
    PTh = [[Buf("PTh%d_%d" % (i, g)) for g in range(2)] for i in range(3)]
    Osb, rz = [], []
    for i in range(4):
        v, o = carve(o, [128, 512], F32)
        Osb.append((v, Buf("Osb%d" % i)))
        v, o = carve(o, [128, 512], F32)
        rz.append((v, Buf("rz%d" % i)))
    onb, sqa, rsa, rsa2 = [], [], [], []
    for i in range(2):
        v, o = carve(o, [128, 512], F32)
        rsa2.append((v, Buf("rsa2_%d" % i)))
        v, o = carve(o, [128, 512], F32)
        onb.append((v, Buf("on%d" % i)))
        v, o = carve(o, [128, 512], BF16)
        sqa.append((v, Buf("sqa%d" % i)))
        v, o = carve(o, [128, 512], F32)
        rsa.append((v, Buf("rsa%d" % i)))
    assert o <= ARENA_KB * K, o

    o = 0
    stg32, stg16 = [], []
    for i in range(2):
        v, o = carve(o, [128, 11 * 512], F32)
        stg32.append((v, Buf("stg32_%d" % i)))
        v, o = carve(o, [128, 11 * 512], BF16)
        stg16.append((v, Buf("stg16_%d" % i)))

    def dma(out, in_, reads, writes, name):
        return S.add("sp", (lambda e: [e.dma_start(out=out, in_=in_)]), reads=reads, writes=writes, dma=name)

    dma(ident, ident_d, [], [Bconst], "c0")
    dma(gcols, gcols_d, [], [Bconst], "c0")
    dma(gbc, gbc_d, [], [Bconst], "c0")
    S.add("dve", lambda e: e.memset(ones, 1.0), writes=[Bconst])
    S.add("dve", lambda e: e.memset(zfill, 0.0), writes=[Bconst])
    S.add("dve", lambda e: e.memset(bones, 0.0), writes=[Bconst])
    S.add("dve", lambda e: e.memset(bones[0:64, 0:64], 1.0), writes=[Bconst])
    S.add("dve", lambda e: e.memset(bones[64:128, 64:128], 1.0), writes=[Bconst])
    lp = lamp.rearrange("p (a d) -> p a d", d=64)
    S.add("dve", lambda e: e.tensor_tensor(out=lp, in0=lamv[:, 0:4:2, :], in1=lamv[:, 1:4:2, :], op=ALU.mult),
          reads=[Bconst], writes=[Bsmall])
    S.add("dve", lambda e: e.tensor_reduce(out=small[:, 2:4], in_=lp, axis=AX.X, op=ALU.add),
          reads=[Bsmall], writes=[Bsmall])
    S.add("act", lambda e: e.activation(out=small[:, 4:6], in_=small[:, 2:4], func=AF.Exp),
          reads=[Bsmall], writes=[Bsmall])
    S.add("dve", lambda e: e.scalar_tensor_tensor(out=nlam, in0=small[:, 5:6], scalar=-LAMBDA_INIT, in1=small[:, 4:5],
                                                  op0=ALU.add, op1=ALU.subtract), reads=[Bsmall], writes=[Bsmall])
    S.add("dve", lambda e: e.tensor_scalar(out=gbs, in0=gcols[:, 28:29], scalar1=1.0 - LAMBDA_INIT, scalar2=None,
                                           op0=ALU.mult), reads=[Bconst, Bsmall], writes=[Bsmall])

    cv = [0]
    wbuf = {}

    def convert(src_view, nk, dst, key):
        i = cv[0] % 4
        cv[0] += 1
        b = Buf("w_%s_%d" % key)
        wbuf[key] = b
        S.add("pool", (lambda e: [e.dma_start(out=dst.rearrange("p (k c) -> p k c", c=512), in_=src_view)]),
              writes=[b], dma="cv%d" % i)

    def conv_rows8(w, blocks, dst, name):
        wv = w.rearrange("(k p) c -> p k c", p=128)
        for b in blocks:
            convert(wv[:, :, b * 512:(b + 1) * 512], 8, dst[b], (name, b))

    def conv_wd(w, dst, name):
        wv = w.rearrange("(f p) c -> p f c", p=128)
        for n in range(2):
            for fh in range(2):
                convert(wv[:, fh * 11:(fh + 1) * 11, n * 512:(n + 1) * 512], 11, dst[n * 2 + fh], (name, n * 2 + fh))

    gu_order = []
    for f in range(NF):
        for b in (f // 4, (NF + f) // 4):
            if b not in gu_order:
                gu_order.append(b)

    def convert_group(g):
        if g == 0:
            conv_rows8(wgu_d[0], gu_order, wgus[0], "gu0")
            conv_wd(wd_d[0], wds[0], "wd0")
            conv_rows8(win_d, range(6), wins, "win")
        else:
            conv_rows8(wout_d, range(2), wouts, "wout")
            conv_rows8(wgu_d[1], gu_order, wgus[1], "gu1")
            conv_wd(wd_d[1], wds[1], "wd1")

    convert_group(0)

    def wblk(ws, b):
        return ws[b].rearrange("p (k c) -> p k c", c=512)

    def plan_ffn_weights(which):
        order = []
        seen = set()
        for f in range(NF):
            for key in (("g", f // 4), ("u", (NF + f) // 4)):
                if key not in seen:
                    seen.add(key)
                    order.append(key)
        wring.plan([(wblk(wgus[which], b), wbuf[("gu%d" % which, b)]) for (_, b) in order])
        dring.plan([(wds[which][i].rearrange("p (f c) -> p f c", c=512), wbuf[("wd%d" % which, i)]) for i in range(4)])
        wring.prefetch(wring.used + 2)
        dring.prefetch(dring.used + 1)
        return order

    sgc = [0]

    def norm_to_hT(xs, gbase):
        xv = xt[xs]
        st = stat_buf[0]
        for j in range(4):
            S.add("act", (lambda e, j=j: e.activation(out=hb[:, j, :], in_=xv[:, j, :], func=AF.Square,
                                                      accum_out=stat[:, j:j + 1])),
                  reads=[xt_buf[xs][j]], writes=[hb_buf[j], st])
        S.add("dve", lambda e: e.tensor_scalar(out=stat[:, 4:8], in0=stat[:, 0:4], scalar1=1.0 / DM, scalar2=EPS,
                                               op0=ALU.mult, op1=ALU.add), reads=[st], writes=[st])
        S.add("act", lambda e: e.activation(out=stat[:, 4:8], in_=stat[:, 4:8], func=AF.Sqrt), reads=[st], writes=[st])
        S.add("dve", lambda e: e.reciprocal(out=stat[:, 8:12], in_=stat[:, 4:8]), reads=[st], writes=[st])
        for j in range(4):
            S.add("act", (lambda e, j=j: e.mul(out=hb[:, j, :], in_=xv[:, j, :], mul=stat[:, 8 + j:9 + j])),
                  reads=[xt_buf[xs][j], st], writes=[hb_buf[j]])
        fb = next_bank()

        def filler(e, fb=fb):
            ins = None
            for _ in range(NFILL):
                ins = e.matmul(bank_ap[fb], lhsT=zfill[:, 0:128], rhs=zfill, start=True, stop=True)
            return ins
        S.add("pe", filler, reads=[Bconst], writes=[bank_buf[fb]])
        for kp in range(4):
            bi = next_bank()
            pb = bank_ap[bi].bitcast(BF16)

            def tr(e, kp=kp, pb=pb):
                ins = None
                for kk in range(2):
                    k = kp * 2 + kk
                    for j in range(4):
                        ins = e.transpose(out=pb[:, kk * 512 + j * 128: kk * 512 + (j + 1) * 128],
                                          in_=hb[:, j, k * 128:(k + 1) * 128], identity=ident)
                return ins
            S.add("pe", tr, reads=hb_buf + [Bconst], writes=[bank_buf[bi]])
            for kk in range(2):
                k = kp * 2 + kk
                S.add("dve", (lambda e, k=k, kk=kk, pb=pb: e.tensor_scalar(
                    out=hT[:, k, :], in0=pb[:, kk * 512:(kk + 1) * 512], scalar1=gcols[:, gbase + k:gbase + k + 1],
                    scalar2=None, op0=ALU.mult)), reads=[bank_buf[bi], Bconst], writes=[hT_buf])
        fb2 = next_bank()

        def filler2(e, fb2=fb2):
            ins = None
            for _ in range(NFILL2):
                ins = e.matmul(bank_ap[fb2], lhsT=zfill[:, 0:128], rhs=zfill, start=True, stop=True)
            return ins
        S.add("pe", filler2, reads=[Bconst], writes=[bank_buf[fb2]])

    planned = [False]

    def ffn(xs, gbase, which):
        if planned[0]:
            planned[0] = False
        else:
            plan_ffn_weights(which)
        norm_to_hT(xs, gbase)
        slot_of = {}
        for f in range(NF):
            for key in (("g", f // 4), ("u", (NF + f) // 4)):
                if key not in slot_of:
                    slot_of[key] = wring.next(ahead=1)
            (wg, wgb, wgi) = slot_of[("g", f // 4)]
            (wu, wub, wui) = slot_of[("u", (NF + f) // 4)]
            wring.check(wgi)
            wring.check(wui)
            og = (f % 4) * 128
            ou = ((NF + f) % 4) * 128
            bg = next_bank()
            bu = next_bank()

            def mmg(e, w=wg, o_=og, b=bg):
                ins = None
                for k in range(8):
                    ins = e.matmul(bank_ap[b], lhsT=w[:, k, o_:o_ + 128], rhs=hT[:, k, :], start=(k == 0), stop=(k == 7))
                return ins
            S.add("pe", mmg, reads=[wgb, hT_buf], writes=[bank_buf[bg]])
            S.add("pe", (lambda e, w=wu, o_=ou, b=bu: mmg(e, w, o_, b)), reads=[wub, hT_buf], writes=[bank_buf[bu]])
            sgv, sgb = sg[sgc[0] % 2]
            sgc[0] += 1
            S.add("act", (lambda e, sgv=sgv, b=bg: e.activation(out=sgv, in_=bank_ap[b], func=AF.Silu)),
                  reads=[bank_buf[bg]], writes=[sgb])
            S.add("dve", (lambda e, sgv=sgv, b=bu, f=f: e.tensor_tensor(out=aT[:, f, :], in0=sgv, in1=bank_ap[b],
                                                                        op=ALU.mult)),
                  reads=[sgb, bank_buf[bu]], writes=[aT_buf[f]])
        for n in range(2):
            banks = [next_bank() for _ in range(4)]
            for fh in range(2):
                wv, wb, _ = dring.next(ahead=1)
                for j in range(4):
                    def mmd(e, wv=wv, j=j, fh=fh, b=banks[j]):
                        ins = None
                        for i in range(11):
                            ins = e.matmul(bank_ap[b], lhsT=aT[:, fh * 11 + i, j * 128:(j + 1) * 128], rhs=wv[:, i, :],
                                           start=(fh == 0 and i == 0), stop=(fh == 1 and i == 10))
                        return ins
                    S.add("pe", mmd, reads=[wb] + aT_buf[fh * 11:(fh + 1) * 11], writes=[bank_buf[banks[j]]])
            for j in range(4):
                xsl = xt[xs][:, j, n * 512:(n + 1) * 512]
                S.add("dve", (lambda e, xsl=xsl, b=banks[j]: e.scalar_tensor_tensor(
                    out=xsl, in0=bank_ap[b], scalar=0.5, in1=xsl, op0=ALU.mult, op1=ALU.add)),
                    reads=[bank_buf[banks[j]], xt_buf[xs][j]], writes=[xt_buf[xs][j]])

    def load_x(src, r0, xs):
        dma(xt[xs], src[r0:r0 + T, :].rearrange("(j p) c -> p j c", p=128), [], xt_buf[xs], "xl%d" % xs)

    conv_done = [False]

    def phase1(tok0, Sq):
        nt = Sq // T
        load_x(x_d, tok0, 0)
        qc = [0]
        qkc = [0]
        for t in range(nt):
            xs = t % 2
            r0 = tok0 + t * T
            if not conv_done[0] and (t == 1 or nt == 1):
                conv_done[0] = True
                convert_group(1)
            cv_, cb_ = cst[t % 2]
            dma(cv_, cs_d[r0:r0 + T, :].rearrange("(j p) c -> p j c", p=128), [], [cb_], "csl%d" % (t % 2))
            if t + 1 < nt:
                load_x(x_d, r0 + T, 1 - xs)
            ffn(xs, 0, 0)
            wring.plan([(wblk(wins, b), wbuf[("win", b)]) for b in range(6)])
            wring.prefetch(wring.used + 2)
            dma(x1s[r0:r0 + T, :].rearrange("(j p) c -> p j c", p=128), xt[xs], xt_buf[xs], [], "x1st%d" % xs)
            norm_to_hT(xs, 8)
            for j in range(4):
                S.add("pool", (lambda e, j=j, cv_=cv_: e.tensor_tensor(
                    out=gcs[:, j, :, :].rearrange("p t (h d) -> p t h d", d=64),
                    in0=cv_[:, j, :].rearrange("p (h d) -> p h d", d=64).unsqueeze(1).to_broadcast([128, 4, 2, 64]),
                    in1=g44, op=ALU.mult)),
                    reads=[cb_, Bconst], writes=[gcs_buf])
            chains = []
            tick = [0]

            def advance():
                tk = tick[0]
                for born, ch in chains:
                    age = tk - born
                    if age in ch:
                        ch[age]()
                while chains and tk - chains[0][0] >= 5:
                    chains.pop(0)
                tick[0] += 1

            for n in range(6):
                wv, wb, _ = wring.next(ahead=2)
                for j in range(4):
                    bi = next_bank()

                    def mmp(e, wv=wv, j=j, bi=bi):
                        ins = None
                        for k in range(8):
                            ins = e.matmul(bank_ap[bi], lhsT=hT[:, k, j * 128:(j + 1) * 128], rhs=wv[:, k, :],
                                           start=(k == 0), stop=(k == 7))
                        return ins
                    S.add("pe", mmp, reads=[wb, hT_buf], writes=[bank_buf[bi]])
                    if n in (2, 5):
                        c0 = 0 if n == 2 else 512
                        S.add("act", (lambda e, j=j, bi=bi, c0=c0: e.copy(out=Vst[:, j, c0:c0 + 512], in_=bank_ap[bi])),
                              reads=[bank_buf[bi]], writes=[Vst_buf])
                        advance()
                        continue
                    ty = {0: 0, 1: 1, 3: 2, 4: 3}[n]
                    kq = qc[0]
                    qc[0] += 1
                    sqv, sqB = sqb[kq % 2]
                    xnv, xnB = xnb[kq % 3]
                    t1v, t1B = t1b[kq % 3]
                    t2v, t2B = t2b[kq % 2]
                    qkv, qkB = qkb[kq % len(qkb)]
                    st = qstat_buf[kq % 3]
                    so = 16 + (kq % 3) * 24
                    gc_ = gcs[:, j, ty, 0:64]
                    gs_ = gcs[:, j, ty, 64:128]
                    x3 = xnv.rearrange("p (g d) -> p g d", d=64)
                    t3 = t2v.rearrange("p (g d) -> p g d", d=64)

                    def st0(bi=bi, sqv=sqv, sqB=sqB, st=st, so=so):
                        S.add("act", (lambda e: e.activation(out=sqv, in_=bank_ap[bi], func=AF.Square)),
                              reads=[bank_buf[bi]], writes=[sqB])
                        S.add("dve", (lambda e: e.tensor_reduce(out=stat[:, so:so + 8], in_=sqv.rearrange("p (g d) -> p g d", d=64),
                                                                axis=AX.X, op=ALU.add)), reads=[sqB], writes=[st])
                        S.add("dve", (lambda e: e.tensor_scalar(out=stat[:, so + 8:so + 16], in0=stat[:, so:so + 8],
                                                                scalar1=1.0 / 64, scalar2=EPS, op0=ALU.mult, op1=ALU.add)),
                              reads=[st], writes=[st])

                    def st1(st=st, so=so):
                        S.add("act", (lambda e: e.activation(out=stat[:, so + 8:so + 16], in_=stat[:, so + 8:so + 16],
                                                             func=AF.Sqrt)), reads=[st], writes=[st])

                    def st2(bi=bi, st=st, so=so, xnv=xnv, xnB=xnB, t1v=t1v, t1B=t1B, gc_=gc_):
                        S.add("dve", (lambda e: e.reciprocal(out=stat[:, so + 16:so + 24], in_=stat[:, so + 8:so + 16])),
                              reads=[st], writes=[st])
                        S.add("dve", (lambda e: e.tensor_tensor(
                            out=xnv.rearrange("p (g d) -> p g d", d=64), in0=bank_ap[bi].rearrange("p (g d) -> p g d", d=64),
                            in1=stat[:, so + 16:so + 24].unsqueeze(2).to_broadcast([128, 8, 64]), op=ALU.mult)),
                            reads=[bank_buf[bi], st], writes=[xnB])
                        S.add("dve", (lambda e: e.tensor_tensor(
                            out=t1v.rearrange("p (g d) -> p g d", d=64), in0=xnv.rearrange("p (g d) -> p g d", d=64),
                            in1=gc_.unsqueeze(1).to_broadcast([128, 8, 64]), op=ALU.mult)),
                            reads=[xnB, gcs_buf], writes=[t1B])

                    def st3(x3=x3, t3=t3, gs_=gs_, xnB=xnB, t2B=t2B):
                        S.add("pool", (lambda e: e.tensor_tensor(out=t3[:, :, 0:32], in0=x3[:, :, 32:64],
                                                                 in1=gs_[:, 0:32].unsqueeze(1).to_broadcast([128, 8, 32]),
                                                                 op=ALU.mult)), reads=[xnB, gcs_buf], writes=[t2B])
                        S.add("pool", (lambda e: e.tensor_tensor(out=t3[:, :, 32:64], in0=x3[:, :, 0:32],
                                                                 in1=gs_[:, 32:64].unsqueeze(1).to_broadcast([128, 8, 32]),
                                                                 op=ALU.mult)), reads=[xnB, gcs_buf], writes=[t2B])

                    def st4(qkv=qkv, qkB=qkB, t1v=t1v, t1B=t1B, t2v=t2v, t2B=t2B):
                        S.add("dve", (lambda e: e.tensor_tensor(out=qkv, in0=t1v, in1=t2v, op=ALU.add)),
                              reads=[t1B, t2B], writes=[qkB])

                    def st5(qkv=qkv, qkB=qkB, ty=ty, j=j):
                        b2 = next_bank()
                        pb = bank_ap[b2].bitcast(BF16)

                        def trq(e):
                            ins = None
                            for cc in range(4):
                                ins = e.transpose(out=pb[:, cc * 128:(cc + 1) * 128], in_=qkv[:, cc * 128:(cc + 1) * 128],
                                                  identity=ident)
                            return ins
                        S.add("pe", trq, reads=[qkB, Bconst], writes=[bank_buf[b2]])
                        S.add("act", (lambda e: e.copy(
                            out=QKst[:, ty * 4:(ty + 1) * 4, j * 128:(j + 1) * 128],
                            in_=pb[:, 0:512].rearrange("p (c t) -> p c t", t=128))),
                            reads=[bank_buf[b2]], writes=[QKst_buf])
                    chains.append((tick[0], {0: st0, 1: st1, 2: st2, 3: st3, 4: st4, 5: st5}))
                    advance()
            if t + 1 < nt:
                plan_ffn_weights(0)
                planned[0] = True
            while chains:
                advance()
            dma(qts[:, :, r0:r0 + T].rearrange("c p t -> p c t"), QKst, [QKst_buf], [], "qst")
            dma(vs[r0:r0 + T, :].rearrange("(j p) c -> p j c", p=128), Vst, [Vst_buf], [], "vst")

    def attention(tok0, Sq):
        NKB = Sq // 128
        NQG = Sq // 512
        ptc = [0]
        sc = [0]
        ozc = [0]
        pc = [0]
        for sl in range(2):
            (q0, q1), k, v, b = att_slots[sl]
            S.add("pool", (lambda e, q0=q0: e.memset(q0[64:128, 0:Sq], 0.0)), writes=[b])
            S.add("pool", (lambda e, q1=q1: e.memset(q1[0:64, 0:Sq], 0.0)), writes=[b])
            S.add("pool", (lambda e, v=v: e.memset(v[:, 0:NKB, 64:128], 1.0)), writes=[b])

        def load_chunk(c, slot):
            (q0, q1), k, v, b = att_slots[slot]
            if c < 4:
                qi, ki, vc = c, 4 + c, c * 128
            else:
                qi, ki, vc = 8 + (c - 4), 12 + (c - 4), 512 + (c - 4) * 128
            vsrc = vs[tok0:tok0 + Sq, :].rearrange("(kb p) c -> p kb c", p=128)
            if c < 4:
                vd = [(v[:, 0:NKB, 0:64], vsrc[:, :, vc:vc + 64]), (v[:, 0:NKB, 128:192], vsrc[:, :, vc + 64:vc + 128])]
            else:
                vd = [(v[:, 0:NKB, 0:128], vsrc[:, :, vc:vc + 128])]
            S.add("sp", (lambda e: [e.dma_start(out=q0[0:64, 0:Sq], in_=qts[qi, 0:64, tok0:tok0 + Sq]),
                                    e.dma_start(out=q1[64:128, 0:Sq], in_=qts[qi, 64:128, tok0:tok0 + Sq]),
                                    e.dma_start(out=k[:, 0:Sq], in_=qts[ki, :, tok0:tok0 + Sq])]
                         + [e.dma_start(out=o_, in_=i_) for (o_, i_) in vd]),
                  writes=[b], dma="att%d" % slot, ndma=3 + len(vd))

        def make_post(c, qg, isA, streams):
            pp = (c * NQG + qg) % 2
            onv, onB = onb[pp]
            sqv, sqB = sqa[pp]
            rsv, rsB = rsa[pp]
            zev, zeB = rsa2[pp]
            (o0, o0B, z0, z0B), (o1, o1B, z1, z1B) = streams
            if isA:
                lhs_ss, inv_n, gcol = bones, 1.0 / 64, gcols[:, 24 + c:25 + c]
            else:
                lhs_ss, inv_n, gcol = ones, 1.0 / 128, gbs
            mo = mixT[:, c, qg * 512:(qg + 1) * 512]

            def s1a():
                if isA:
                    S.add("dve", (lambda e: e.scalar_tensor_tensor(out=zev[0:64, :], in0=z0[0:64, :], scalar=EPS,
                                                                   in1=z0[0:64, :], op0=ALU.mult, op1=ALU.mult)),
                          reads=[z0B], writes=[zeB])
                    S.add("dve", (lambda e: e.scalar_tensor_tensor(out=zev[64:128, :], in0=z1[64:128, :], scalar=EPS,
                                                                   in1=z1[64:128, :], op0=ALU.mult, op1=ALU.mult)),
                          reads=[z1B], writes=[zeB])
                else:
                    S.add("dve", (lambda e: e.reciprocal(out=z1, in_=z1)), reads=[z1B], writes=[z1B])
                    S.add("dve", (lambda e: e.tensor_tensor(out=z1, in0=z1, in1=z0, op=ALU.mult)), reads=[z1B, z0B], writes=[z1B])
                    S.add("dve", (lambda e: e.tensor_tensor(out=o1, in0=o1, in1=z1, op=ALU.mult)), reads=[o1B, z1B], writes=[o1B])
                    S.add("dve", (lambda e: e.scalar_tensor_tensor(out=onv, in0=o1, scalar=nlam, in1=o0, op0=ALU.mult,
                                                                   op1=ALU.add)), reads=[o0B, o1B, Bsmall], writes=[onB])
                    S.add("dve", (lambda e: e.scalar_tensor_tensor(out=zev, in0=z0, scalar=EPS, in1=z0, op0=ALU.mult,
                                                                   op1=ALU.mult)), reads=[z0B], writes=[zeB])

            def s1b():
                if isA:
                    S.add("dve", (lambda e: e.tensor_tensor(out=sqv[0:64, :], in0=o0[0:64, :], in1=o0[0:64, :], op=ALU.mult)),
                          reads=[o0B], writes=[sqB])
                    S.add("dve", (lambda e: e.tensor_tensor(out=sqv[64:128, :], in0=o1[64:128, :], in1=o1[64:128, :],
                                                            op=ALU.mult)), reads=[o1B], writes=[sqB])
                else:
                    S.add("act", (lambda e: e.activation(out=sqv, in_=onv, func=AF.Square)), reads=[onB], writes=[sqB])

            def s2a():
                sp_ = 2 * (sc[0] % (3 if isA else 2))
                sc[0] += 1
                S.add("pe", (lambda e: e.matmul(bank_ap[sp_], lhsT=lhs_ss, rhs=sqv, start=True, stop=True)),
                      reads=[sqB, Bconst], writes=[bank_buf[sp_]])
                S.add("dve", (lambda e: e.scalar_tensor_tensor(out=rsv, in0=bank_ap[sp_], scalar=inv_n, in1=zev, op0=ALU.mult,
                                                               op1=ALU.add)), reads=[bank_buf[sp_], zeB], writes=[rsB])

            def s2b():
                S.add("act", (lambda e: e.activation(out=rsv, in_=rsv, func=AF.Ln)), reads=[rsB], writes=[rsB])
                S.add("act", (lambda e: e.activation(out=rsv, in_=rsv, func=AF.Exp, scale=-0.5)), reads=[rsB], writes=[rsB])

            def s2c():
                if isA:
                    S.add("dve", (lambda e: e.scalar_tensor_tensor(out=mo[0:64, :], in0=o0[0:64, :], scalar=gcol[0:64, :],
                                                                   in1=rsv[0:64, :], op0=ALU.mult, op1=ALU.mult)),
                          reads=[o0B, rsB, Bconst], writes=[mix_buf[c][qg]])
                    S.add("dve", (lambda e: e.scalar_tensor_tensor(out=mo[64:128, :], in0=o1[64:128, :],
                                                                   scalar=gcol[64:128, :], in1=rsv[64:128, :],
                                                                   op0=ALU.mult, op1=ALU.mult)),
                          reads=[o1B, rsB, Bconst], writes=[mix_buf[c][qg]])
                else:
                    S.add("dve", (lambda e: e.scalar_tensor_tensor(out=mo, in0=onv, scalar=gcol, in1=rsv, op0=ALU.mult,
                                                                   op1=ALU.mult)),
                          reads=[onB, rsB, Bconst, Bsmall], writes=[mix_buf[c][qg]])
            if isA:
                return [(1, s1b), (3, s1a), (5, s2a), (6, s2b), (7, s2c)]
            return [(0, s1a), (6, s1b), (8, s2a), (10, s2b), (12, s2c)]

        def make_steps(chunks):
            steps = []
            for c in chunks:
                isA = c < 4
                for qg in range(NQG):
                    for e_ in range(2):
                        if isA:
                            kb_lo, kb_hi = max(0, 4 * qg - 8), min(NKB - 1, 4 * qg + 11)
                        else:
                            kb_lo, kb_hi = 0, NKB - 1
                        kbs = list(range(kb_lo, kb_hi + 1))
                        grps = [kbs[i:i + 2] for i in range(0, len(kbs), 2)]
                        for gi, grp in enumerate(grps):
                            steps.append(dict(c=c, qg=qg, e=e_, grp=grp, first=(gi == 0), last=(gi == len(grps) - 1), isA=isA))
            return steps

        state = {"streams": [], "oz": None}
        deferred = []

        def front_qk(st):
            c, qg, e_, grp = st["c"], st["qg"], st["e"], st["grp"]
            Qs, KT, V, ab = att_slots[c % 2]
            Qe = Qs[e_]
            ng = len(grp)
            sp_ = 2 * (sc[0] % (3 if st["isA"] else 2))
            sc[0] += 1
            sb = [bank_buf[sp_ + g] for g in range(ng)]

            def qk(e):
                ins = None
                for g, kb in enumerate(grp):
                    ins = e.matmul(bank_ap[sp_ + g], lhsT=KT[:, kb * 128:(kb + 1) * 128],
                                   rhs=Qe[:, qg * 512:(qg + 1) * 512], start=True, stop=True)
                return ins
            S.add("pe", qk, reads=[ab], writes=sb)
            st["sp"] = (sp_, sb)

        def front(st):
            if "sp" not in st:
                front_qk(st)
            c, qg, e_, grp = st["c"], st["qg"], st["e"], st["grp"]
            ng = len(grp)
            sp_, sb = st["sp"]
            pslot = ptc[0] % 3
            pv_, pB = PT[pslot]
            ptc[0] += 1
            if st["isA"]:
                S.add("act", (lambda e: e.activation(out=pv_[:, 0:ng * 512], in_=ps_all[:, sp_ * 512:(sp_ + ng) * 512],
                                                     func=AF.Exp, scale=0.125)), reads=sb, writes=[pB])
            else:
                for g in range(ng):
                    S.add("act", (lambda e, g=g: e.activation(out=pv_[:, g * 512:(g + 1) * 512], in_=bank_ap[sp_ + g],
                                                              func=AF.Exp, scale=0.125)),
                          reads=[sb[g]], writes=([pB] if g == 0 else []) + [PTh[pslot][g]])
            st["pth"] = PTh[pslot]
            if st["isA"]:
                rel = grp[0] - 4 * qg + 8
                S.add("dve", (lambda e: e.tensor_tensor(out=pv_[:, 0:ng * 512], in0=pv_[:, 0:ng * 512],
                                                        in1=amask[:, rel:rel + ng, :].rearrange("p r q -> p (r q)"),
                                                        op=ALU.mult)), reads=[pB, Bconst], writes=[pB])
            st["pt"] = (pv_, pB)

        def back(st, idx):
            c, qg, e_, grp = st["c"], st["qg"], st["e"], st["grp"]
            if st["first"] and e_ == 0 and qg == 0 and c < 7:
                load_chunk(c + 1, (c + 1) % 2)
            Qs, KT, V, ab = att_slots[c % 2]
            pv_, pB = st["pt"]
            isA_ = st["isA"]
            if st["first"]:
                if isA_:
                    state["oz"] = 6 + (ozc[0] % 2)
                else:
                    state["oz"] = 4 + 2 * (ozc[0] % 2)
                ozc[0] += 1
            Ob = state["oz"]
            Zb = Ob + 1
            fst, lst = st["first"], st["last"]
            ng = len(grp)

            if isA_:
                def pvm(e):
                    ins = None
                    for g, kb in enumerate(grp):
                        ins = e.matmul(bank_ap[Ob], lhsT=V[:, kb, e_ * 64:e_ * 64 + 128], rhs=pv_[:, g * 512:(g + 1) * 512],
                                       start=(fst and g == 0), stop=(lst and g == ng - 1))
                    return ins
                S.add("pe", pvm, reads=[pB, ab], writes=[bank_buf[Ob]])
            else:
                for g, kb in enumerate(grp):
                    def pvm(e, g=g, kb=kb):
                        a_ = fst and g == 0
                        z_ = lst and g == ng - 1
                        e.matmul(bank_ap[Ob], lhsT=V[:, kb, 0:128], rhs=pv_[:, g * 512:(g + 1) * 512], start=a_, stop=z_)
                        return e.matmul(bank_ap[Zb], lhsT=ones, rhs=pv_[:, g * 512:(g + 1) * 512], start=a_, stop=z_)
                    S.add("pe", pvm, reads=[st["pth"][g], ab, Bconst], writes=[bank_buf[Ob], bank_buf[Zb]])
            if lst:
                pi = pc[0] % 4
                pc[0] += 1
                ov, oB = Osb[pi]
                rv, rB = rz[pi]
                if isA_:
                    S.add("dve", (lambda e: e.tensor_copy(out=ov, in_=bank_ap[Ob])), reads=[bank_buf[Ob]], writes=[oB])
                else:
                    S.add("act", (lambda e: e.copy(out=ov, in_=bank_ap[Ob])), reads=[bank_buf[Ob]], writes=[oB])
                if isA_:
                    if e_ == 0:
                        S.add("sp", (lambda e: [e.dma_start(out=rv[0:64, :], in_=ov[64:128, :])]), reads=[oB], writes=[rB],
                              dma="zmv%d" % (pi % 2))
                    else:
                        S.add("sp", (lambda e: [e.dma_start(out=rv[64:128, :], in_=ov[0:64, :])]), reads=[oB], writes=[rB],
                              dma="zmv%d" % (pi % 2))
                else:
                    S.add("dve", (lambda e: e.tensor_copy(out=rv, in_=bank_ap[Zb])), reads=[bank_buf[Zb]], writes=[rB])
                state["streams"].append((ov, oB, rv, rB))
                if e_ == 1:
                    strs = state["streams"]
                    state["streams"] = []
                    cap = 3 * ((min(NKB, 12) + 1) // 2) - 1
                    for dly, fn in make_post(c, qg, st["isA"], strs):
                        deferred.append((idx + min(dly, cap), fn))
                    deferred.sort(key=lambda t: t[0])

        load_chunk(0, 0)
        dma(amask, amask_d.rearrange("p (r q) -> p r q", q=512), [], [Bconst], "c0")
        for chunks, LAG in ((range(0, 4), 2), (range(4, 8), 1)):
            steps = make_steps(chunks)
            n = len(steps)
            for i in range(n + LAG):
                if LAG == 2 and i + 1 < n:
                    if i == 0:
                        front_qk(steps[0])
                    front_qk(steps[i + 1])
                if i < n:
                    front(steps[i])
                if i - LAG >= 0:
                    back(steps[i - LAG], i - LAG)
                while deferred and deferred[0][0] <= i - LAG:
                    deferred.pop(0)[1]()
            while deferred:
                deferred.pop(0)[1]()
        if debug:
            for c in range(8):
                dma(mixd[c, :, tok0:tok0 + Sq], mixT[:, c, 0:Sq], [mix_buf[c][g] for g in range(NQG)], [], "dbg")

    def phase3(tok0, Sq):
        nt = Sq // T
        load_x(x1s, tok0, 0)
        for t in range(nt):
            xs = t % 2
            r0 = tok0 + t * T
            if t + 1 < nt:
                load_x(x1s, r0 + T, 1 - xs)
            if t == 0:
                wring.plan([(wblk(wouts, b), wbuf[("wout", b)]) for b in range(2)])
                wring.prefetch(wring.used + 2)
            for n in range(2):
                wv, wb, _ = wring.next(ahead=1)
                for j in range(4):
                    bi = next_bank()
                    tk = t * T + j * 128

                    def mmo(e, wv=wv, tk=tk, bi=bi):
                        ins = None
                        for k in range(8):
                            ins = e.matmul(bank_ap[bi], lhsT=mixT[:, k, tk:tk + 128], rhs=wv[:, k, :],
                                           start=(k == 0), stop=(k == 7))
                        return ins
                    S.add("pe", mmo, reads=[wb] + [mix_buf[k][t] for k in range(8)], writes=[bank_buf[bi]])
                    xsl = xt[xs][:, j, n * 512:(n + 1) * 512]
                    S.add("dve", (lambda e, xsl=xsl, bi=bi: e.tensor_tensor(out=xsl, in0=bank_ap[bi], in1=xsl, op=ALU.add)),
                          reads=[bank_buf[bi], xt_buf[xs][j]], writes=[xt_buf[xs][j]])
            ffn(xs, 16, 1)
            if t + 1 < nt:
                wring.plan([(wblk(wouts, b), wbuf[("wout", b)]) for b in range(2)])
                wring.prefetch(wring.used + 2)
            st = stat_buf[3]
            for j in range(4):
                S.add("act", (lambda e, j=j, xs=xs: e.activation(out=hb[:, j, :], in_=xt[xs][:, j, :], func=AF.Square,
                                                                 accum_out=stat[:, 12 + j:13 + j])),
                      reads=[xt_buf[xs][j]], writes=[hb_buf[j], st])
            S.add("dve", lambda e: e.tensor_scalar(out=stat[:, 16:20], in0=stat[:, 12:16], scalar1=1.0 / DM, scalar2=EPS,
                                                   op0=ALU.mult, op1=ALU.add), reads=[st], writes=[st])
            S.add("act", lambda e: e.activation(out=stat[:, 16:20], in_=stat[:, 16:20], func=AF.Sqrt),
                  reads=[st], writes=[st])
            S.add("dve", lambda e: e.reciprocal(out=stat[:, 20:24], in_=stat[:, 16:20]), reads=[st], writes=[st])
            for j in range(4):
                S.add("dve", (lambda e, j=j, xs=xs: e.scalar_tensor_tensor(
                    out=xt[xs][:, j, :], in0=xt[xs][:, j, :], scalar=stat[:, 20 + j:21 + j], in1=gfin,
                    op0=ALU.mult, op1=ALU.mult)), reads=[xt_buf[xs][j], st, Bconst], writes=[xt_buf[xs][j]])
            dma(y_d[r0:r0 + T, :].rearrange("(j p) c -> p j c", p=128), xt[xs], xt_buf[xs], [], "yst%d" % xs)

    tok0 = 0
    for Sq in seqs:
        phase1(tok0, Sq)
        S.barrier()
        attention(tok0, Sq)
        S.barrier()
        phase3(tok0, Sq)
        S.barrier()
        tok0 += Sq
    S.emit_all(nc)
    return nc


def _mult(delta):
    d = np.abs(delta)
    w = (d <= 64).astype(np.float32)
    w += ((delta % 4 == 0) & (d <= 256)).astype(np.float32)
    w += ((delta % 16 == 0) & (d <= 1024)).astype(np.float32)
    return w


def host_consts(seqs):
    cs = []
    inv = 1.0 / (10000.0 ** (np.arange(0, 64, 2, dtype=np.float32) / np.float32(64)))
    for Sq in seqs:
        ang = np.arange(Sq, dtype=np.float32)[:, None] * inv[None, :].astype(np.float32)
        ang = np.concatenate([ang, ang], axis=-1)
        c = np.cos(ang).astype(np.float32)
        s = np.sin(ang).astype(np.float32)
        s = np.concatenate([-s[:, :32], s[:, 32:]], axis=-1)
        cs.append(np.concatenate([c, s], axis=-1))
    cs = np.ascontiguousarray(np.concatenate(cs, axis=0), dtype=np.float32)
    kk = np.arange(128)[:, None]
    qq = np.arange(512)[None, :]
    am = np.zeros((128, 20, 512), np.float32)
    for r in range(20):
        rel = r - 8
        delta = 128 * rel + kk - qq
        am[:, r, :] = _mult(delta)
    ident = np.eye(128, dtype=np.float32).astype(ml_dtypes.bfloat16)
    return cs, am.reshape(128, 20 * 512).astype(ml_dtypes.bfloat16), ident


def host_small(inp):
    f = lambda a: np.asarray(a, np.float32)
    gcols = np.zeros((128, 32), np.float32)
    gcols[:, 0:8] = f(inp["g_ffn1"]).reshape(8, 128).T
    gcols[:, 8:16] = f(inp["g_mix"]).reshape(8, 128).T
    gcols[:, 16:24] = f(inp["g_ffn2"]).reshape(8, 128).T
    gcols[:, 24:28] = f(inp["g_a_out"]).reshape(4, 128).T
    gcols[:, 28] = f(inp["g_b_out"]).reshape(128)
    def gsw(a):
        a = f(a).reshape(-1)
        return np.concatenate([a, a[32:], a[:32]])
    row = np.concatenate([f(inp["g_final"]).reshape(-1), gsw(inp["g_a_q"]), gsw(inp["g_a_k"]),
                          gsw(inp["g_b_q"]), gsw(inp["g_b_k"]), f(inp["lam_q1"]).reshape(-1),
                          f(inp["lam_k1"]).reshape(-1), f(inp["lam_q2"]).reshape(-1), f(inp["lam_k2"]).reshape(-1)])
    gbc = np.ascontiguousarray(np.broadcast_to(row[None, :], (128, row.size)))
    return gcols, gbc


_CACHE = {}


def run(xs_per_core, inp, seqs, debug=False, trace=False):
    key = (tuple(seqs), debug)
    if key not in _CACHE:
        _CACHE[key] = build_program(list(seqs), debug=debug)
    nc = _CACHE[key]
    cs, am, ident = host_consts(seqs)
    gcols, gbc = host_small(inp)
    f = lambda a: np.ascontiguousarray(np.asarray(a, np.float32))
    common = {
        "w_gu1": f(inp["w_ffn1_gu"][0]), "w_gu2": f(inp["w_ffn2_gu"][0]),
        "w_d1": f(inp["w_ffn1_down"][0]), "w_d2": f(inp["w_ffn2_down"][0]),
        "w_in": f(inp["w_in"][0]), "w_out": f(inp["w_out"][0]),
        "gcols": gcols, "gbc": gbc, "cs": cs, "ident": ident, "amask": am,
    }
    in_maps = [dict(common, x=f(x)) for x in xs_per_core]
    res = run_bass_kernel_spmd(nc, in_maps, core_ids=list(range(len(in_maps))), trace=trace)
    return res


def kernel(**inputs):
    xp = np.asarray(inputs["x_prompt"], np.float32)
    xsm = np.asarray(inputs["x_sample"], np.float32)
    n = 8
    seqs = (xp.shape[1], xsm.shape[1])
    xs = [np.concatenate([xp[c], xsm[c]], axis=0) for c in range(n)]
    res = run(xs, inputs, seqs)
    ys = [r["y"] for r in res.results]
    yp = np.stack([y[:seqs[0]] for y in ys], axis=0).astype(np.float32)
    ysm = np.stack([y[seqs[0]:] for y in ys], axis=0).astype(np.float32)
    return (yp, ysm)
```

```python
import math
from collections import deque
from contextlib import ExitStack

import numpy as np
import ml_dtypes
import concourse.bass as bass
import concourse.mybir as mybir
from concourse.bass_utils import run_bass_kernel_spmd

F32 = mybir.dt.float32
BF16 = mybir.dt.bfloat16
AF = mybir.ActivationFunctionType
ALU = mybir.AluOpType
AX = mybir.AxisListType

DM = 1024
DFF = 2816
NF = 22
T = 512
EPS = 1e-6
NFILL = 36
NFILL2 = 10
LAMBDA_INIT = 0.8 - 0.6 * math.exp(-0.3 * 0)


class Buf:
    __slots__ = ("name", "w", "r")

    def __init__(self, name):
        self.name = name
        self.w = None
        self.r = {}


class Op:
    __slots__ = ("eng", "emit", "deps", "sig", "seq", "dsem", "dbase", "dend", "idx")


COMPUTE = ("pe", "act", "dve", "pool")
ALLENG = ("pe", "act", "dve", "pool", "sp")


class Sched:
    def __init__(self):
        self.streams = {e: [] for e in ALLENG}
        self.dma_tot = {}
        self.nops = 0
        self.barrier_deps = {}
        self.last_dma = {}

    def add(self, eng, emit, reads=(), writes=(), dma=None, ndma=1):
        op = Op()
        op.eng = eng
        op.emit = emit
        op.sig = False
        op.seq = None
        op.idx = self.nops
        self.nops += 1
        deps = {}
        for b in reads:
            if b.w is not None:
                deps[id(b.w)] = (b.w, True)
        for b in writes:
            if b.w is not None:
                deps[id(b.w)] = (b.w, True)
            for r in b.r.values():
                if id(r) not in deps:
                    deps[id(r)] = (r, False)
        bd = self.barrier_deps.pop(eng, None)
        if bd:
            for d in bd:
                deps[id(d)] = (d, True)
        if dma is not None:
            op.dsem = dma
            op.dbase = self.dma_tot.get(dma, 0)
            op.dend = op.dbase + 16 * ndma
            self.dma_tot[dma] = op.dend
            self.last_dma[dma] = op
        else:
            op.dsem = None
            op.dbase = op.dend = 0
        final = []
        for d, strong in deps.values():
            if d is op:
                continue
            if d.dsem is None and d.eng == eng:
                if eng == "pe":
                    continue
                if not strong:
                    continue
            final.append(d)
            if d.dsem is None:
                d.sig = True
        op.deps = final
        wset = set(id(b) for b in writes)
        for b in writes:
            b.w = op
            b.r = {}
        key = eng if dma is None else ("dma", dma)
        for b in reads:
            if id(b) not in wset:
                b.r[key] = op
        self.streams[eng].append(op)
        return op

    def barrier(self):
        deps = []
        for e in ALLENG:
            for op in reversed(self.streams[e]):
                if op.dsem is None:
                    deps.append(op)
                    break
        deps.extend(self.last_dma.values())
        for d in deps:
            if d.dsem is None:
                d.sig = True
        for e in ALLENG:
            self.barrier_deps[e] = self.barrier_deps.get(e, []) + deps

    def emit_all(self, nc):
        for e in COMPUTE:
            c = 0
            for op in self.streams[e]:
                if op.sig and op.dsem is None:
                    c += 1
                    op.seq = c
        with ExitStack() as es:
            esem = {e: es.enter_context(nc.semaphore("sem_" + e)) for e in COMPUTE}
            dsem = {n: es.enter_context(nc.semaphore("dsem_" + n)) for n in self.dma_tot}
            block = es.enter_context(nc.Block())
            sched = self

            def run(ename, eng):
                waited = {}

                def wait(key, sem, val):
                    if val <= 0:
                        return
                    if waited.get(key, 0) < val:
                        eng.wait_ge(sem, val)
                        waited[key] = val

                for op in sched.streams[ename]:
                    for d in op.deps:
                        if d.dsem is None:
                            wait(d.eng, esem[d.eng], d.seq)
                        else:
                            wait(("d", d.dsem), dsem[d.dsem], d.dend)
                    if op.dsem is not None:
                        wait(("d", op.dsem), dsem[op.dsem], op.dbase)
                        for ins in op.emit(eng):
                            ins.then_inc(dsem[op.dsem], 16)
                    else:
                        ins = op.emit(eng)
                        if op.sig:
                            ins.then_inc(esem[ename], 1)
                if ename == "sp":
                    for n, tot in sched.dma_tot.items():
                        wait(("d", n), dsem[n], tot)

            @block.tensor
            def _(e):
                run("pe", e)

            @block.scalar
            def _(e):
                run("act", e)

            @block.vector
            def _(e):
                run("dve", e)

            @block.gpsimd
            def _(e):
                run("pool", e)

            @block.sync
            def _(e):
                run("sp", e)


class Ring:
    def __init__(self, S, slots, name):
        self.S = S
        self.slots = slots
        self.n = len(slots)
        self.name = name
        self.pending = deque()
        self.issued = 0
        self.used = 0

    def plan(self, srcs):
        self.pending.extend(srcs)

    def prefetch(self, upto):
        while self.issued < upto and self.pending:
            src, sbuf = self.pending.popleft()
            i = self.issued % self.n
            ap, buf = self.slots[i]
            self.S.add("sp", (lambda e, ap=ap, src=src: [e.dma_start(out=ap, in_=src)]),
                       reads=[sbuf], writes=[buf], dma="%s%d" % (self.name, i))
            self.issued += 1

    def next(self, ahead):
        i = self.used
        self.used += 1
        self.prefetch(i + 1 + ahead)
        assert self.issued > i
        ap, buf = self.slots[i % self.n]
        return ap, buf, i

    def check(self, i):
        assert self.issued <= i + self.n, "ring slot overwritten while live"


def build_program(seqs, debug=False):
    NTOK = sum(seqs)
    nc = bass.Bass("TRN2", target_bir_lowering=False)
    S = Sched()

    def din(name, shape, dt=F32):
        return nc.dram_tensor(name, list(shape), dt, kind="ExternalInput").ap()

    x_d = din("x", [NTOK, DM])
    wgu_d = [din("w_gu1", [DM, 2 * DFF]), din("w_gu2", [DM, 2 * DFF])]
    wd_d = [din("w_d1", [DFF, DM]), din("w_d2", [DFF, DM])]
    win_d = din("w_in", [DM, 3072])
    wout_d = din("w_out", [DM, DM])
    gcols_d = din("gcols", [128, 32])
    gbc_d = din("gbc", [128, 1024 + 512 + 256])
    cs_d = din("cs", [NTOK, 128])
    ident_d = din("ident", [128, 128], BF16)
    amask_d = din("amask", [128, 20 * 512], BF16)
    y_d = nc.dram_tensor("y", [NTOK, DM], F32, kind="ExternalOutput").ap()

    ikind = "ExternalOutput" if debug else "Internal"
    x1s = nc.dram_tensor("x1s", [NTOK, DM], F32, kind=ikind).ap()
    qts = nc.dram_tensor("qts", [16, 128, NTOK], BF16, kind=ikind).ap()
    vs = nc.dram_tensor("vs", [NTOK, 1024], BF16, kind=ikind).ap()
    mixd = nc.dram_tensor("mixd", [8, 128, NTOK], BF16, kind=ikind).ap() if debug else None
    wgus = [nc.dram_tensor("wgus%d" % i, [11, 128, 8 * 512], BF16, kind="Internal").ap() for i in range(2)]
    wds = [nc.dram_tensor("wds%d" % i, [4, 128, 11 * 512], BF16, kind="Internal").ap() for i in range(2)]
    wins = nc.dram_tensor("wins", [6, 128, 8 * 512], BF16, kind="Internal").ap()
    wouts = nc.dram_tensor("wouts", [2, 128, 8 * 512], BF16, kind="Internal").ap()

    ARENA_KB = 192
    arena = nc.alloc_sbuf_tensor("arena", [128, ARENA_KB * 256], F32).ap()

    def carve(off, shape, dt):
        n = 1
        for s in shape[1:]:
            n *= s
        nbytes = n * (4 if dt == F32 else 2)
        assert off % 4 == 0 and nbytes % 4 == 0
        assert off + nbytes <= ARENA_KB * 1024, (off, nbytes)
        v = arena[:, off // 4:(off + nbytes) // 4]
        if dt != F32:
            v = v.bitcast(dt)
        if len(shape) == 3:
            v = v.rearrange("p (a b) -> p a b", b=shape[2])
        elif len(shape) == 4:
            v = v.rearrange("p (a b c) -> p a b c", b=shape[2], c=shape[3])
        return v, off + nbytes

    ident = nc.alloc_sbuf_tensor("sb_ident", [128, 128], BF16).ap()
    ones = nc.alloc_sbuf_tensor("sb_ones", [128, 128], BF16).ap()
    bones = nc.alloc_sbuf_tensor("sb_bones", [128, 128], BF16).ap()
    zfill = nc.alloc_sbuf_tensor("sb_zfill", [128, 512], BF16).ap()
    gcols = nc.alloc_sbuf_tensor("sb_gcols", [128, 32], F32).ap()
    gbc = nc.alloc_sbuf_tensor("sb_gbc", [128, 1792], F32).ap()
    small = nc.alloc_sbuf_tensor("sb_small", [128, 64], F32).ap()
    lamp = nc.alloc_sbuf_tensor("sb_lamp", [128, 128], F32).ap()
    Bconst = Buf("const")
    Bsmall = Buf("small")
    gfin = gbc[:, 0:1024]
    g44 = gbc[:, 1024:1536].rearrange("p (t h d) -> p t h d", h=2, d=64)
    lamv = gbc[:, 1536:1792].rearrange("p (t d) -> p t d", d=64)
    nlam = small[:, 0:1]
    gbs = small[:, 1:2]

    ps_all = nc.alloc_psum_tensor("ps", [128, 4096], F32).ap()
    bank_ap = [ps_all[:, i * 512:(i + 1) * 512] for i in range(8)]
    bank_buf = [Buf("bank%d" % i) for i in range(8)]
    bank_ctr = [0]

    def next_bank():
        i = bank_ctr[0] % 8
        bank_ctr[0] += 1
        return i

    K = 1024
    o = 64 * K
    xt = []
    for i in range(2):
        v, o = carve(o, [128, 4, 1024], F32)
        xt.append(v)
    xt_buf = [[Buf("xt%d_%d" % (i, j)) for j in range(4)] for i in range(2)]
    hb, o = carve(o, [128, 4, 1024], BF16)
    hb_buf = [Buf("hb%d" % j) for j in range(4)]
    hT, o = carve(o, [128, 8, 512], BF16)
    hT_buf = Buf("hT")
    aT, o = carve(o, [128, NF, 512], BF16)
    aT_buf = [Buf("aT%d" % f) for f in range(NF)]
    wr_slots = []
    for i in range(4):
        v, o = carve(o, [128, 8, 512], BF16)
        wr_slots.append((v, Buf("wr%d" % i)))
    wd_slots = []
    for i in range(2):
        v, o = carve(o, [128, 11, 512], BF16)
        wd_slots.append((v, Buf("wd%d" % i)))
    sg = []
    for i in range(2):
        v, o = carve(o, [128, 512], BF16)
        sg.append((v, Buf("sg%d" % i)))
    ffn_end = o
    assert ffn_end <= ARENA_KB * K
    wring = Ring(S, wr_slots, "wr")
    dring = Ring(S, wd_slots, "wd")

    o = 0
    QKst, o = carve(o, [128, 16, 512], BF16)
    QKst_buf = Buf("QKst")
    Vst, o = carve(o, [128, 4, 1024], BF16)
    Vst_buf = Buf("Vst")
    cst = []
    for i in range(2):
        v, o = carve(o, [128, 4, 128], F32)
        cst.append((v, Buf("cst%d" % i)))
    gcs, o = carve(o, [128, 4, 4, 128], F32)
    gcs_buf = Buf("gcs")
    sqb, xnb, t1b, t2b, qkb = [], [], [], [], []
    NQS = 3
    for i in range(NQS):
        v, o = carve(o, [128, 512], F32)
        sqb.append((v, Buf("sqb%d" % i)))
        v, o = carve(o, [128, 512], F32)
        xnb.append((v, Buf("xnb%d" % i)))
        v, o = carve(o, [128, 512], F32)
        t1b.append((v, Buf("t1b%d" % i)))
        v, o = carve(o, [128, 512], F32)
        t2b.append((v, Buf("t2b%d" % i)))
        v, o = carve(o, [128, 512], BF16)
        qkb.append((v, Buf("qkb%d" % i)))
    p1_end = o
    assert p1_end <= 64 * K, p1_end
    o = ffn_end
    for i in range(2):
        v, o = carve(o, [128, 512], BF16)
        qkb.append((v, Buf("qkb%d" % (NQS + i))))
    stat = nc.alloc_sbuf_tensor("sb_stat", [128, 96], F32).ap()
    stat_buf = [Buf("stat%d" % i) for i in range(4)]
    qstat_buf = [Buf("qstat%d" % i) for i in range(3)]

    SMAX = max(seqs)
    mixT, _ = carve(0, [128, 8, SMAX], BF16)
    assert 8 * SMAX * 2 <= 64 * K
    mix_buf = [[Buf("mix%d_%d" % (c, g)) for g in range(SMAX // 512)] for c in range(8)]
    o = 64 * K
    att_slots = []
    for i in range(2):
        q0, o = carve(o, [128, SMAX], BF16)
        q1, o = carve(o, [128, SMAX], BF16)
        k, o = carve(o, [128, SMAX], BF16)
        v, o = carve(o, [128, SMAX // 128, 192], BF16)
        att_slots.append(((q0, q1), k, v, Buf("att%d" % i)))
    amask, o = carve(o, [128, 20, 512], BF16)
    PT = []
    for i in range(3):
        v, o = carve(o, [128, 1024], BF16)
        PT.append((v, Buf("PT%d" % i)))
    PTh = [[Buf("PTh%d_%d" % (i, g)) for g in range(2)] for i in range(3)]
    Osb, rz = [], []
    for i in range(4):
        v, o = carve(o, [128, 512], F32)
        Osb.append((v, Buf("Osb%d" % i)))
        v, o = carve(o, [128, 512], F32)
        rz.append((v, Buf("rz%d" % i)))
    onb, sqa, rsa, rsa2 = [], [], [], []
    for i in range(2):
        v, o = carve(o, [128, 512], F32)
        rsa2.append((v, Buf("rsa2_%d" % i)))
        v, o = carve(o, [128, 512], F32)
        onb.append((v, Buf("on%d" % i)))
        v, o = carve(o, [128, 512], BF16)
        sqa.append((v, Buf("sqa%d" % i)))
        v, o = carve(o, [128, 512], F32)
        rsa.append((v, Buf("rsa%d" % i)))
    assert o <= ARENA_KB * K, o

    o = 0
    stg32, stg16 = [], []
    for i in range(2):
        v, o = carve(o, [128, 11 * 512], F32)
        stg32.append((v, Buf("stg32_%d" % i)))
        v, o = carve(o, [128, 11 * 512], BF16)
        stg16.append((v, Buf("stg16_%d" % i)))

    def dma(out, in_, reads, writes, name):
        return S.add("sp", (lambda e: [e.dma_start(out=out, in_=in_)]), reads=reads, writes=writes, dma=name)

    dma(ident, ident_d, [], [Bconst], "c0")
    dma(gcols, gcols_d, [], [Bconst], "c0")
    dma(gbc, gbc_d, [], [Bconst], "c0")
    S.add("dve", lambda e: e.memset(ones, 1.0), writes=[Bconst])
    S.add("dve", lambda e: e.memset(zfill, 0.0), writes=[Bconst])
    S.add("dve", lambda e: e.memset(bones, 0.0), writes=[Bconst])
    S.add("dve", lambda e: e.memset(bones[0:64, 0:64], 1.0), writes=[Bconst])
    S.add("dve", lambda e: e.memset(bones[64:128, 64:128], 1.0), writes=[Bconst])
    lp = lamp.rearrange("p (a d) -> p a d", d=64)
    S.add("dve", lambda e: e.tensor_tensor(out=lp, in0=lamv[:, 0:4:2, :], in1=lamv[:, 1:4:2, :], op=ALU.mult),
          reads=[Bconst], writes=[Bsmall])
    S.add("dve", lambda e: e.tensor_reduce(out=small[:, 2:4], in_=lp, axis=AX.X, op=ALU.add),
          reads=[Bsmall], writes=[Bsmall])
    S.add("act", lambda e: e.activation(out=small[:, 4:6], in_=small[:, 2:4], func=AF.Exp),
          reads=[Bsmall], writes=[Bsmall])
    S.add("dve", lambda e: e.scalar_tensor_tensor(out=nlam, in0=small[:, 5:6], scalar=-LAMBDA_INIT, in1=small[:, 4:5],
                                                  op0=ALU.add, op1=ALU.subtract), reads=[Bsmall], writes=[Bsmall])
    S.add("dve", lambda e: e.tensor_scalar(out=gbs, in0=gcols[:, 28:29], scalar1=1.0 - LAMBDA_INIT, scalar2=None,
                                           op0=ALU.mult), reads=[Bconst, Bsmall], writes=[Bsmall])

    cv = [0]
    wbuf = {}

    def convert(src_view, nk, dst, key):
        i = cv[0] % 4
        cv[0] += 1
        b = Buf("w_%s_%d" % key)
        wbuf[key] = b
        S.add("pool", (lambda e: [e.dma_start(out=dst.rearrange("p (k c) -> p k c", c=512), in_=src_view)]),
              writes=[b], dma="cv%d" % i)

    def conv_rows8(w, blocks, dst, name):
        wv = w.rearrange("(k p) c -> p k c", p=128)
        for b in blocks:
            convert(wv[:, :, b * 512:(b + 1) * 512], 8, dst[b], (name, b))

    def conv_wd(w, dst, name):
        wv = w.rearrange("(f p) c -> p f c", p=128)
        for n in range(2):
            for fh in range(2):
                convert(wv[:, fh * 11:(fh + 1) * 11, n * 512:(n + 1) * 512], 11, dst[n * 2 + fh], (name, n * 2 + fh))

    gu_order = []
    for f in range(NF):
        for b in (f // 4, (NF + f) // 4):
            if b not in gu_order:
                gu_order.append(b)

    def convert_group(g):
        if g == 0:
            conv_rows8(wgu_d[0], gu_order, wgus[0], "gu0")
            conv_wd(wd_d[0], wds[0], "wd0")
            conv_rows8(win_d, range(6), wins, "win")
        else:
            conv_rows8(wout_d, range(2), wouts, "wout")
            conv_rows8(wgu_d[1], gu_order, wgus[1], "gu1")
            conv_wd(wd_d[1], wds[1], "wd1")

    convert_group(0)

    def wblk(ws, b):
        return ws[b].rearrange("p (k c) -> p k c", c=512)

    def plan_ffn_weights(which):
        order = []
        seen = set()
        for f in range(NF):
            for key in (("g", f // 4), ("u", (NF + f) // 4)):
                if key not in seen:
                    seen.add(key)
                    order.append(key)
        wring.plan([(wblk(wgus[which], b), wbuf[("gu%d" % which, b)]) for (_, b) in order])
        dring.plan([(wds[which][i].rearrange("p (f c) -> p f c", c=512), wbuf[("wd%d" % which, i)]) for i in range(4)])
        wring.prefetch(wring.used + 2)
        dring.prefetch(dring.used + 1)
        return order

    sgc = [0]

    def norm_to_hT(xs, gbase):
        xv = xt[xs]
        st = stat_buf[0]
        for j in range(4):
            S.add("act", (lambda e, j=j: e.activation(out=hb[:, j, :], in_=xv[:, j, :], func=AF.Square,
                                                      accum_out=stat[:, j:j + 1])),
                  reads=[xt_buf[xs][j]], writes=[hb_buf[j], st])
        S.add("dve", lambda e: e.tensor_scalar(out=stat[:, 4:8], in0=stat[:, 0:4], scalar1=1.0 / DM, scalar2=EPS,
                                               op0=ALU.mult, op1=ALU.add), reads=[st], writes=[st])
        S.add("act", lambda e: e.activation(out=stat[:, 4:8], in_=stat[:, 4:8], func=AF.Sqrt), reads=[st], writes=[st])
        S.add("dve", lambda e: e.reciprocal(out=stat[:, 8:12], in_=stat[:, 4:8]), reads=[st], writes=[st])
        for j in range(4):
            S.add("act", (lambda e, j=j: e.mul(out=hb[:, j, :], in_=xv[:, j, :], mul=stat[:, 8 + j:9 + j])),
                  reads=[xt_buf[xs][j], st], writes=[hb_buf[j]])
        fb = next_bank()

        def filler(e, fb=fb):
            ins = None
            for _ in range(NFILL):
                ins = e.matmul(bank_ap[fb], lhsT=zfill[:, 0:128], rhs=zfill, start=True, stop=True)
            return ins
        S.add("pe", filler, reads=[Bconst], writes=[bank_buf[fb]])
        for kp in range(4):
            bi = next_bank()
            pb = bank_ap[bi].bitcast(BF16)

            def tr(e, kp=kp, pb=pb):
                ins = None
                for kk in range(2):
                    k = kp * 2 + kk
                    for j in range(4):
                        ins = e.transpose(out=pb[:, kk * 512 + j * 128: kk * 512 + (j + 1) * 128],
                                          in_=hb[:, j, k * 128:(k + 1) * 128], identity=ident)
                return ins
            S.add("pe", tr, reads=hb_buf + [Bconst], writes=[bank_buf[bi]])
            for kk in range(2):
                k = kp * 2 + kk
                S.add("dve", (lambda e, k=k, kk=kk, pb=pb: e.tensor_scalar(
                    out=hT[:, k, :], in0=pb[:, kk * 512:(kk + 1) * 512], scalar1=gcols[:, gbase + k:gbase + k + 1],
                    scalar2=None, op0=ALU.mult)), reads=[bank_buf[bi], Bconst], writes=[hT_buf])
        fb2 = next_bank()

        def filler2(e, fb2=fb2):
            ins = None
            for _ in range(NFILL2):
                ins = e.matmul(bank_ap[fb2], lhsT=zfill[:, 0:128], rhs=zfill, start=True, stop=True)
            return ins
        S.add("pe", filler2, reads=[Bconst], writes=[bank_buf[fb2]])

    planned = [False]

    def ffn(xs, gbase, which):
        if planned[0]:
            planned[0] = False
        else:
            plan_ffn_weights(which)
        norm_to_hT(xs, gbase)
        slot_of = {}
        for f in range(NF):
            for key in (("g", f // 4), ("u", (NF + f) // 4)):
                if key not in slot_of:
                    slot_of[key] = wring.next(ahead=1)
            (wg, wgb, wgi) = slot_of[("g", f // 4)]
            (wu, wub, wui) = slot_of[("u", (NF + f) // 4)]
            wring.check(wgi)
            wring.check(wui)
            og = (f % 4) * 128
            ou = ((NF + f) % 4) * 128
            bg = next_bank()
            bu = next_bank()

            def mmg(e, w=wg, o_=og, b=bg):
                ins = None
                for k in range(8):
                    ins = e.matmul(bank_ap[b], lhsT=w[:, k, o_:o_ + 128], rhs=hT[:, k, :], start=(k == 0), stop=(k == 7))
                return ins
            S.add("pe", mmg, reads=[wgb, hT_buf], writes=[bank_buf[bg]])
            S.add("pe", (lambda e, w=wu, o_=ou, b=bu: mmg(e, w, o_, b)), reads=[wub, hT_buf], writes=[bank_buf[bu]])
            sgv, sgb = sg[sgc[0] % 2]
            sgc[0] += 1
            S.add("act", (lambda e, sgv=sgv, b=bg: e.activation(out=sgv, in_=bank_ap[b], func=AF.Silu)),
                  reads=[bank_buf[bg]], writes=[sgb])
            S.add("dve", (lambda e, sgv=sgv, b=bu, f=f: e.tensor_tensor(out=aT[:, f, :], in0=sgv, in1=bank_ap[b],
                                                                        op=ALU.mult)),
                  reads=[sgb, bank_buf[bu]], writes=[aT_buf[f]])
        for n in range(2):
            banks = [next_bank() for _ in range(4)]
            for fh in range(2):
                wv, wb, _ = dring.next(ahead=1)
                for j in range(4):
                    def mmd(e, wv=wv, j=j, fh=fh, b=banks[j]):
                        ins = None
                        for i in range(11):
                            ins = e.matmul(bank_ap[b], lhsT=aT[:, fh * 11 + i, j * 128:(j + 1) * 128], rhs=wv[:, i, :],
                                           start=(fh == 0 and i == 0), stop=(fh == 1 and i == 10))
                        return ins
                    S.add("pe", mmd, reads=[wb] + aT_buf[fh * 11:(fh + 1) * 11], writes=[bank_buf[banks[j]]])
            for j in range(4):
                xsl = xt[xs][:, j, n * 512:(n + 1) * 512]
                S.add("dve", (lambda e, xsl=xsl, b=banks[j]: e.scalar_tensor_tensor(
                    out=xsl, in0=bank_ap[b], scalar=0.5, in1=xsl, op0=ALU.mult, op1=ALU.add)),
                    reads=[bank_buf[banks[j]], xt_buf[xs][j]], writes=[xt_buf[xs][j]])

    def load_x(src, r0, xs):
        dma(xt[xs], src[r0:r0 + T, :].rearrange("(j p) c -> p j c", p=128), [], xt_buf[xs], "xl%d" % xs)

    conv_done = [False]

    def phase1(tok0, Sq):
        nt = Sq // T
        load_x(x_d, tok0, 0)
        qc = [0]
        qkc = [0]
        for t in range(nt):
            xs = t % 2
            r0 = tok0 + t * T
            if not conv_done[0] and (t == 1 or nt == 1):
                conv_done[0] = True
                convert_group(1)
            cv_, cb_ = cst[t % 2]
            dma(cv_, cs_d[r0:r0 + T, :].rearrange("(j p) c -> p j c", p=128), [], [cb_], "csl%d" % (t % 2))
            if t + 1 < nt:
                load_x(x_d, r0 + T, 1 - xs)
            ffn(xs, 0, 0)
            wring.plan([(wblk(wins, b), wbuf[("win", b)]) for b in range(6)])
            wring.prefetch(wring.used + 2)
            dma(x1s[r0:r0 + T, :].rearrange("(j p) c -> p j c", p=128), xt[xs], xt_buf[xs], [], "x1st%d" % xs)
            norm_to_hT(xs, 8)
            for j in range(4):
                S.add("pool", (lambda e, j=j, cv_=cv_: e.tensor_tensor(
                    out=gcs[:, j, :, :].rearrange("p t (h d) -> p t h d", d=64),
                    in0=cv_[:, j, :].rearrange("p (h d) -> p h d", d=64).unsqueeze(1).to_broadcast([128, 4, 2, 64]),
                    in1=g44, op=ALU.mult)),
                    reads=[cb_, Bconst], writes=[gcs_buf])
            chains = []
            tick = [0]

            def advance():
                tk = tick[0]
                for born, ch in chains:
                    age = tk - born
                    if age in ch:
                        ch[age]()
                while chains and tk - chains[0][0] >= 5:
                    chains.pop(0)
                tick[0] += 1

            for n in range(6):
                wv, wb, _ = wring.next(ahead=2)
                for j in range(4):
                    bi = next_bank()

                    def mmp(e, wv=wv, j=j, bi=bi):
                        ins = None
                        for k in range(8):
                            ins = e.matmul(bank_ap[bi], lhsT=hT[:, k, j * 128:(j + 1) * 128], rhs=wv[:, k, :],
                                           start=(k == 0), stop=(k == 7))
                        return ins
                    S.add("pe", mmp, reads=[wb, hT_buf], writes=[bank_buf[bi]])
                    if n in (2, 5):
                        c0 = 0 if n == 2 else 512
                        S.add("act", (lambda e, j=j, bi=bi, c0=c0: e.copy(out=Vst[:, j, c0:c0 + 512], in_=bank_ap[bi])),
                              reads=[bank_buf[bi]], writes=[Vst_buf])
                        advance()
                        continue
                    ty = {0: 0, 1: 1, 3: 2, 4: 3}[n]
                    kq = qc[0]
                    qc[0] += 1
                    sqv, sqB = sqb[kq % 2]
                    xnv, xnB = xnb[kq % 3]
                    t1v, t1B = t1b[kq % 3]
                    t2v, t2B = t2b[kq % 2]
                    qkv, qkB = qkb[kq % len(qkb)]
                    st = qstat_buf[kq % 3]
                    so = 16 + (kq % 3) * 24
                    gc_ = gcs[:, j, ty, 0:64]
                    gs_ = gcs[:, j, ty, 64:128]
                    x3 = xnv.rearrange("p (g d) -> p g d", d=64)
                    t3 = t2v.rearrange("p (g d) -> p g d", d=64)

                    def st0(bi=bi, sqv=sqv, sqB=sqB, st=st, so=so):
                        S.add("act", (lambda e: e.activation(out=sqv, in_=bank_ap[bi], func=AF.Square)),
                              reads=[bank_buf[bi]], writes=[sqB])
                        S.add("dve", (lambda e: e.tensor_reduce(out=stat[:, so:so + 8], in_=sqv.rearrange("p (g d) -> p g d", d=64),
                                                                axis=AX.X, op=ALU.add)), reads=[sqB], writes=[st])
                        S.add("dve", (lambda e: e.tensor_scalar(out=stat[:, so + 8:so + 16], in0=stat[:, so:so + 8],
                                                                scalar1=1.0 / 64, scalar2=EPS, op0=ALU.mult, op1=ALU.add)),
                              reads=[st], writes=[st])

                    def st1(st=st, so=so):
                        S.add("act", (lambda e: e.activation(out=stat[:, so + 8:so + 16], in_=stat[:, so + 8:so + 16],
                                                             func=AF.Sqrt)), reads=[st], writes=[st])

                    def st2(bi=bi, st=st, so=so, xnv=xnv, xnB=xnB, t1v=t1v, t1B=t1B, gc_=gc_):
                        S.add("dve", (lambda e: e.reciprocal(out=stat[:, so + 16:so + 24], in_=stat[:, so + 8:so + 16])),
                              reads=[st], writes=[st])
                        S.add("dve", (lambda e: e.tensor_tensor(
                            out=xnv.rearrange("p (g d) -> p g d", d=64), in0=bank_ap[bi].rearrange("p (g d) -> p g d", d=64),
                            in1=stat[:, so + 16:so + 24].unsqueeze(2).to_broadcast([128, 8, 64]), op=ALU.mult)),
                            reads=[bank_buf[bi], st], writes=[xnB])
                        S.add("dve", (lambda e: e.tensor_tensor(
                            out=t1v.rearrange("p (g d) -> p g d", d=64), in0=xnv.rearrange("p (g d) -> p g d", d=64),
                            in1=gc_.unsqueeze(1).to_broadcast([128, 8, 64]), op=ALU.mult)),
                            reads=[xnB, gcs_buf], writes=[t1B])

                    def st3(x3=x3, t3=t3, gs_=gs_, xnB=xnB, t2B=t2B):
                        S.add("pool", (lambda e: e.tensor_tensor(out=t3[:, :, 0:32], in0=x3[:, :, 32:64],
                                                                 in1=gs_[:, 0:32].unsqueeze(1).to_broadcast([128, 8, 32]),
                                                                 op=ALU.mult)), reads=[xnB, gcs_buf], writes=[t2B])
                        S.add("pool", (lambda e: e.tensor_tensor(out=t3[:, :, 32:64], in0=x3[:, :, 0:32],
                                                                 in1=gs_[:, 32:64].unsqueeze(1).to_broadcast([128, 8, 32]),
                                                                 op=ALU.mult)), reads=[xnB, gcs_buf], writes=[t2B])

                    def st4(qkv=qkv, qkB=qkB, t1v=t1v, t1B=t1B, t2v=t2v, t2B=t2B):
                        S.add("dve", (lambda e: e.tensor_tensor(out=qkv, in0=t1v, in1=t2v, op=ALU.add)),
                              reads=[t1B, t2B], writes=[qkB])

                    def st5(qkv=qkv, qkB=qkB, ty=ty, j=j):
                        b2 = next_bank()
                        pb = bank_ap[b2].bitcast(BF16)

                        def trq(e):
                            ins = None
                            for cc in range(4):
                                ins = e.transpose(out=pb[:, cc * 128:(cc + 1) * 128], in_=qkv[:, cc * 128:(cc + 1) * 128],
                                                  identity=ident)
                            return ins
                        S.add("pe", trq, reads=[qkB, Bconst], writes=[bank_buf[b2]])
                        S.add("act", (lambda e: e.copy(
                            out=QKst[:, ty * 4:(ty + 1) * 4, j * 128:(j + 1) * 128],
                            in_=pb[:, 0:512].rearrange("p (c t) -> p c t", t=128))),
                            reads=[bank_buf[b2]], writes=[QKst_buf])
                    chains.append((tick[0], {0: st0, 1: st1, 2: st2, 3: st3, 4: st4, 5: st5}))
                    advance()
            if t + 1 < nt:
                plan_ffn_weights(0)
                planned[0] = True
            while chains:
                advance()
            dma(qts[:, :, r0:r0 + T].rearrange("c p t -> p c t"), QKst, [QKst_buf], [], "qst")
            dma(vs[r0:r0 + T, :].rearrange("(j p) c -> p j c", p=128), Vst, [Vst_buf], [], "vst")

    def attention(tok0, Sq):
        NKB = Sq // 128
        NQG = Sq // 512
        ptc = [0]
        sc = [0]
        ozc = [0]
        pc = [0]
        for sl in range(2):
            (q0, q1), k, v, b = att_slots[sl]
            S.add("pool", (lambda e, q0=q0: e.memset(q0[64:128, 0:Sq], 0.0)), writes=[b])
            S.add("pool", (lambda e, q1=q1: e.memset(q1[0:64, 0:Sq], 0.0)), writes=[b])
            S.add("pool", (lambda e, v=v: e.memset(v[:, 0:NKB, 64:128], 1.0)), writes=[b])

        def load_chunk(c, slot):
            (q0, q1), k, v, b = att_slots[slot]
            if c < 4:
                qi, ki, vc = c, 4 + c, c * 128
            else:
                qi, ki, vc = 8 + (c - 4), 12 + (c - 4), 512 + (c - 4) * 128
            vsrc = vs[tok0:tok0 + Sq, :].rearrange("(kb p) c -> p kb c", p=128)
            if c < 4:
                vd = [(v[:, 0:NKB, 0:64], vsrc[:, :, vc:vc + 64]), (v[:, 0:NKB, 128:192], vsrc[:, :, vc + 64:vc + 128])]
            else:
                vd = [(v[:, 0:NKB, 0:128], vsrc[:, :, vc:vc + 128])]
            S.add("sp", (lambda e: [e.dma_start(out=q0[0:64, 0:Sq], in_=qts[qi, 0:64, tok0:tok0 + Sq]),
                                    e.dma_start(out=q1[64:128, 0:Sq], in_=qts[qi, 64:128, tok0:tok0 + Sq]),
                                    e.dma_start(out=k[:, 0:Sq], in_=qts[ki, :, tok0:tok0 + Sq])]
                         + [e.dma_start(out=o_, in_=i_) for (o_, i_) in vd]),
                  writes=[b], dma="att%d" % slot, ndma=3 + len(vd))

        def make_post(c, qg, isA, streams):
            pp = (c * NQG + qg) % 2
            onv, onB = onb[pp]
            sqv, sqB = sqa[pp]
            rsv, rsB = rsa[pp]
            zev, zeB = rsa2[pp]
            (o0, o0B, z0, z0B), (o1, o1B, z1, z1B) = streams
            if isA:
                lhs_ss, inv_n, gcol = bones, 1.0 / 64, gcols[:, 24 + c:25 + c]
            else:
                lhs_ss, inv_n, gcol = ones, 1.0 / 128, gbs
            mo = mixT[:, c, qg * 512:(qg + 1) * 512]

            def s1a():
                if isA:
                    S.add("dve", (lambda e: e.scalar_tensor_tensor(out=zev[0:64, :], in0=z0[0:64, :], scalar=EPS,
                                                                   in1=z0[0:64, :], op0=ALU.mult, op1=ALU.mult)),
                          reads=[z0B], writes=[zeB])
                    S.add("dve", (lambda e: e.scalar_tensor_tensor(out=zev[64:128, :], in0=z1[64:128, :], scalar=EPS,
                                                                   in1=z1[64:128, :], op0=ALU.mult, op1=ALU.mult)),
                          reads=[z1B], writes=[zeB])
                else:
                    S.add("dve", (lambda e: e.reciprocal(out=z1, in_=z1)), reads=[z1B], writes=[z1B])
                    S.add("dve", (lambda e: e.tensor_tensor(out=z1, in0=z1, in1=z0, op=ALU.mult)), reads=[z1B, z0B], writes=[z1B])
                    S.add("dve", (lambda e: e.tensor_tensor(out=o1, in0=o1, in1=z1, op=ALU.mult)), reads=[o1B, z1B], writes=[o1B])
                    S.add("dve", (lambda e: e.scalar_tensor_tensor(out=onv, in0=o1, scalar=nlam, in1=o0, op0=ALU.mult,
                                                                   op1=ALU.add)), reads=[o0B, o1B, Bsmall], writes=[onB])
                    S.add("dve", (lambda e: e.scalar_tensor_tensor(out=zev, in0=z0, scalar=EPS, in1=z0, op0=ALU.mult,
                                                                   op1=ALU.mult)), reads=[z0B], writes=[zeB])

            def s1b():
                if isA:
                    S.add("dve", (lambda e: e.tensor_tensor(out=sqv[0:64, :], in0=o0[0:64, :], in1=o0[0:64, :], op=ALU.mult)),
                          reads=[o0B], writes=[sqB])
                    S.add("dve", (lambda e: e.tensor_tensor(out=sqv[64:128, :], in0=o1[64:128, :], in1=o1[64:128, :],
                                                            op=ALU.mult)), reads=[o1B], writes=[sqB])
                else:
                    S.add("act", (lambda e: e.activation(out=sqv, in_=onv, func=AF.Square)), reads=[onB], writes=[sqB])

            def s2a():
                sp_ = 2 * (sc[0] % (3 if isA else 2))
                sc[0] += 1
                S.add("pe", (lambda e: e.matmul(bank_ap[sp_], lhsT=lhs_ss, rhs=sqv, start=True, stop=True)),
                      reads=[sqB, Bconst], writes=[bank_buf[sp_]])
                S.add("dve", (lambda e: e.scalar_tensor_tensor(out=rsv, in0=bank_ap[sp_], scalar=inv_n, in1=zev, op0=ALU.mult,
                                                               op1=ALU.add)), reads=[bank_buf[sp_], zeB], writes=[rsB])

            def s2b():
                S.add("act", (lambda e: e.activation(out=rsv, in_=rsv, func=AF.Ln)), reads=[rsB], writes=[rsB])
                S.add("act", (lambda e: e.activation(out=rsv, in_=rsv, func=AF.Exp, scale=-0.5)), reads=[rsB], writes=[rsB])

            def s2c():
                if isA:
                    S.add("dve", (lambda e: e.scalar_tensor_tensor(out=mo[0:64, :], in0=o0[0:64, :], scalar=gcol[0:64, :],
                                                                   in1=rsv[0:64, :], op0=ALU.mult, op1=ALU.mult)),
                          reads=[o0B, rsB, Bconst], writes=[mix_buf[c][qg]])
                    S.add("dve", (lambda e: e.scalar_tensor_tensor(out=mo[64:128, :], in0=o1[64:128, :],
                                                                   scalar=gcol[64:128, :], in1=rsv[64:128, :],
                                                                   op0=ALU.mult, op1=ALU.mult)),
                          reads=[o1B, rsB, Bconst], writes=[mix_buf[c][qg]])
                else:
                    S.add("dve", (lambda e: e.scalar_tensor_tensor(out=mo, in0=onv, scalar=gcol, in1=rsv, op0=ALU.mult,
                                                                   op1=ALU.mult)),
                          reads=[onB, rsB, Bconst, Bsmall], writes=[mix_buf[c][qg]])
            if isA:
                return [(1, s1b), (3, s1a), (5, s2a), (6, s2b), (7, s2c)]
            return [(0, s1a), (6, s1b), (8, s2a), (10, s2b), (12, s2c)]

        def make_steps(chunks):
            steps = []
            for c in chunks:
                isA = c < 4
                for qg in range(NQG):
                    for e_ in range(2):
                        if isA:
                            kb_lo, kb_hi = max(0, 4 * qg - 8), min(NKB - 1, 4 * qg + 11)
                        else:
                            kb_lo, kb_hi = 0, NKB - 1
                        kbs = list(range(kb_lo, kb_hi + 1))
                        grps = [kbs[i:i + 2] for i in range(0, len(kbs), 2)]
                        for gi, grp in enumerate(grps):
                            steps.append(dict(c=c, qg=qg, e=e_, grp=grp, first=(gi == 0), last=(gi == len(grps) - 1), isA=isA))
            return steps

        state = {"streams": [], "oz": None}
        deferred = []

        def front(st):
            c, qg, e_, grp = st["c"], st["qg"], st["e"], st["grp"]
            Qs, KT, V, ab = att_slots[c % 2]
            Qe = Qs[e_]
            ng = len(grp)
            sp_ = 2 * (sc[0] % (3 if st["isA"] else 2))
            sc[0] += 1
            sb = [bank_buf[sp_ + g] for g in range(ng)]

            def qk(e):
                ins = None
                for g, kb in enumerate(grp):
                    ins = e.matmul(bank_ap[sp_ + g], lhsT=KT[:, kb * 128:(kb + 1) * 128],
                                   rhs=Qe[:, qg * 512:(qg + 1) * 512], start=True, stop=True)
                return ins
            S.add("pe", qk, reads=[ab], writes=sb)
            pslot = ptc[0] % 3
            pv_, pB = PT[pslot]
            ptc[0] += 1
            if st["isA"]:
                S.add("act", (lambda e: e.activation(out=pv_[:, 0:ng * 512], in_=ps_all[:, sp_ * 512:(sp_ + ng) * 512],
                                                     func=AF.Exp, scale=0.125)), reads=sb, writes=[pB])
            else:
                for g in range(ng):
                    S.add("act", (lambda e, g=g: e.activation(out=pv_[:, g * 512:(g + 1) * 512], in_=bank_ap[sp_ + g],
                                                              func=AF.Exp, scale=0.125)),
                          reads=[sb[g]], writes=([pB] if g == 0 else []) + [PTh[pslot][g]])
            st["pth"] = PTh[pslot]
            if st["isA"]:
                rel = grp[0] - 4 * qg + 8
                S.add("dve", (lambda e: e.tensor_tensor(out=pv_[:, 0:ng * 512], in0=pv_[:, 0:ng * 512],
                                                        in1=amask[:, rel:rel + ng, :].rearrange("p r q -> p (r q)"),
                                                        op=ALU.mult)), reads=[pB, Bconst], writes=[pB])
            st["pt"] = (pv_, pB)

        def back(st, idx):
            c, qg, e_, grp = st["c"], st["qg"], st["e"], st["grp"]
            if st["first"] and e_ == 0 and qg == 0 and c < 7:
                load_chunk(c + 1, (c + 1) % 2)
            Qs, KT, V, ab = att_slots[c % 2]
            pv_, pB = st["pt"]
            isA_ = st["isA"]
            if st["first"]:
                if isA_:
                    state["oz"] = 6 + (ozc[0] % 2)
                else:
                    state["oz"] = 4 + 2 * (ozc[0] % 2)
                ozc[0] += 1
            Ob = state["oz"]
            Zb = Ob + 1
            fst, lst = st["first"], st["last"]
            ng = len(grp)

            if isA_:
                def pvm(e):
                    ins = None
                    for g, kb in enumerate(grp):
                        ins = e.matmul(bank_ap[Ob], lhsT=V[:, kb, e_ * 64:e_ * 64 + 128], rhs=pv_[:, g * 512:(g + 1) * 512],
                                       start=(fst and g == 0), stop=(lst and g == ng - 1))
                    return ins
                S.add("pe", pvm, reads=[pB, ab], writes=[bank_buf[Ob]])
            else:
                for g, kb in enumerate(grp):
                    def pvm(e, g=g, kb=kb):
                        a_ = fst and g == 0
                        z_ = lst and g == ng - 1
                        e.matmul(bank_ap[Ob], lhsT=V[:, kb, 0:128], rhs=pv_[:, g * 512:(g + 1) * 512], start=a_, stop=z_)
                        return e.matmul(bank_ap[Zb], lhsT=ones, rhs=pv_[:, g * 512:(g + 1) * 512], start=a_, stop=z_)
                    S.add("pe", pvm, reads=[st["pth"][g], ab, Bconst], writes=[bank_buf[Ob], bank_buf[Zb]])
            if lst:
                pi = pc[0] % 4
                pc[0] += 1
                ov, oB = Osb[pi]
                rv, rB = rz[pi]
                if isA_:
                    S.add("dve", (lambda e: e.tensor_copy(out=ov, in_=bank_ap[Ob])), reads=[bank_buf[Ob]], writes=[oB])
                else:
                    S.add("act", (lambda e: e.copy(out=ov, in_=bank_ap[Ob])), reads=[bank_buf[Ob]], writes=[oB])
                if isA_:
                    if e_ == 0:
                        S.add("sp", (lambda e: [e.dma_start(out=rv[0:64, :], in_=ov[64:128, :])]), reads=[oB], writes=[rB],
                              dma="zmv%d" % (pi % 2))
                    else:
                        S.add("sp", (lambda e: [e.dma_start(out=rv[64:128, :], in_=ov[0:64, :])]), reads=[oB], writes=[rB],
                              dma="zmv%d" % (pi % 2))
                else:
                    S.add("dve", (lambda e: e.tensor_copy(out=rv, in_=bank_ap[Zb])), reads=[bank_buf[Zb]], writes=[rB])
                state["streams"].append((ov, oB, rv, rB))
                if e_ == 1:
                    strs = state["streams"]
                    state["streams"] = []
                    cap = 3 * ((min(NKB, 12) + 1) // 2) - 1
                    for dly, fn in make_post(c, qg, st["isA"], strs):
                        deferred.append((idx + min(dly, cap), fn))
                    deferred.sort(key=lambda t: t[0])

        load_chunk(0, 0)
        dma(amask, amask_d.rearrange("p (r q) -> p r q", q=512), [], [Bconst], "c0")
        for chunks, LAG in ((range(0, 4), 2), (range(4, 8), 1)):
            steps = make_steps(chunks)
            n = len(steps)
            for i in range(n + LAG):
                if i < n:
                    front(steps[i])
                if i - LAG >= 0:
                    back(steps[i - LAG], i - LAG)
                while deferred and deferred[0][0] <= i - LAG:
                    deferred.pop(0)[1]()
            while deferred:
                deferred.pop(0)[1]()
        if debug:
            for c in range(8):
                dma(mixd[c, :, tok0:tok0 + Sq], mixT[:, c, 0:Sq], [mix_buf[c][g] for g in range(NQG)], [], "dbg")

    def phase3(tok0, Sq):
        nt = Sq // T
        load_x(x1s, tok0, 0)
        for t in range(nt):
            xs = t % 2
            r0 = tok0 + t * T
            if t + 1 < nt:
                load_x(x1s, r0 + T, 1 - xs)
            if t == 0:
                wring.plan([(wblk(wouts, b), wbuf[("wout", b)]) for b in range(2)])
                wring.prefetch(wring.used + 2)
            for n in range(2):
                wv, wb, _ = wring.next(ahead=1)
                for j in range(4):
                    bi = next_bank()
                    tk = t * T + j * 128

                    def mmo(e, wv=wv, tk=tk, bi=bi):
                        ins = None
                        for k in range(8):
                            ins = e.matmul(bank_ap[bi], lhsT=mixT[:, k, tk:tk + 128], rhs=wv[:, k, :],
                                           start=(k == 0), stop=(k == 7))
                        return ins
                    S.add("pe", mmo, reads=[wb] + [mix_buf[k][t] for k in range(8)], writes=[bank_buf[bi]])
                    xsl = xt[xs][:, j, n * 512:(n + 1) * 512]
                    S.add("dve", (lambda e, xsl=xsl, bi=bi: e.tensor_tensor(out=xsl, in0=bank_ap[bi], in1=xsl, op=ALU.add)),
                          reads=[bank_buf[bi], xt_buf[xs][j]], writes=[xt_buf[xs][j]])
            ffn(xs, 16, 1)
            if t + 1 < nt:
                wring.plan([(wblk(wouts, b), wbuf[("wout", b)]) for b in range(2)])
                wring.prefetch(wring.used + 2)
            st = stat_buf[3]
            for j in range(4):
                S.add("act", (lambda e, j=j, xs=xs: e.activation(out=hb[:, j, :], in_=xt[xs][:, j, :], func=AF.Square,
                                                                 accum_out=stat[:, 12 + j:13 + j])),
                      reads=[xt_buf[xs][j]], writes=[hb_buf[j], st])
            S.add("dve", lambda e: e.tensor_scalar(out=stat[:, 16:20], in0=stat[:, 12:16], scalar1=1.0 / DM, scalar2=EPS,
                                                   op0=ALU.mult, op1=ALU.add), reads=[st], writes=[st])
            S.add("act", lambda e: e.activation(out=stat[:, 16:20], in_=stat[:, 16:20], func=AF.Sqrt),
                  reads=[st], writes=[st])
            S.add("dve", lambda e: e.reciprocal(out=stat[:, 20:24], in_=stat[:, 16:20]), reads=[st], writes=[st])
            for j in range(4):
                S.add("dve", (lambda e, j=j, xs=xs: e.scalar_tensor_tensor(
                    out=xt[xs][:, j, :], in0=xt[xs][:, j, :], scalar=stat[:, 20 + j:21 + j], in1=gfin,
                    op0=ALU.mult, op1=ALU.mult)), reads=[xt_buf[xs][j], st, Bconst], writes=[xt_buf[xs][j]])
            dma(y_d[r0:r0 + T, :].rearrange("(j p) c -> p j c", p=128), xt[xs], xt_buf[xs], [], "yst%d" % xs)

    tok0 = 0
    for Sq in seqs:
        phase1(tok0, Sq)
        S.barrier()
        attention(tok0, Sq)
        S.barrier()
        phase3(tok0, Sq)
        S.barrier()
        tok0 += Sq
    S.emit_all(nc)
    return nc


def _mult(delta):
    d = np.abs(delta)
    w = (d <= 64).astype(np.float32)
    w += ((delta % 4 == 0) & (d <= 256)).astype(np.float32)
    w += ((delta % 16 == 0) & (d <= 1024)).astype(np.float32)
    return w


def host_consts(seqs):
    cs = []
    inv = 1.0 / (10000.0 ** (np.arange(0, 64, 2, dtype=np.float32) / np.float32(64)))
    for Sq in seqs:
        ang = np.arange(Sq, dtype=np.float32)[:, None] * inv[None, :].astype(np.float32)
        ang = np.concatenate([ang, ang], axis=-1)
        c = np.cos(ang).astype(np.float32)
        s = np.sin(ang).astype(np.float32)
        s = np.concatenate([-s[:, :32], s[:, 32:]], axis=-1)
        cs.append(np.concatenate([c, s], axis=-1))
    cs = np.ascontiguousarray(np.concatenate(cs, axis=0), dtype=np.float32)
    kk = np.arange(128)[:, None]
    qq = np.arange(512)[None, :]
    am = np.zeros((128, 20, 512), np.float32)
    for r in range(20):
        rel = r - 8
        delta = 128 * rel + kk - qq
        am[:, r, :] = _mult(delta)
    ident = np.eye(128, dtype=np.float32).astype(ml_dtypes.bfloat16)
    return cs, am.reshape(128, 20 * 512).astype(ml_dtypes.bfloat16), ident


def host_small(inp):
    f = lambda a: np.asarray(a, np.float32)
    gcols = np.zeros((128, 32), np.float32)
    gcols[:, 0:8] = f(inp["g_ffn1"]).reshape(8, 128).T
    gcols[:, 8:16] = f(inp["g_mix"]).reshape(8, 128).T
    gcols[:, 16:24] = f(inp["g_ffn2"]).reshape(8, 128).T
    gcols[:, 24:28] = f(inp["g_a_out"]).reshape(4, 128).T
    gcols[:, 28] = f(inp["g_b_out"]).reshape(128)
    def gsw(a):
        a = f(a).reshape(-1)
        return np.concatenate([a, a[32:], a[:32]])
    row = np.concatenate([f(inp["g_final"]).reshape(-1), gsw(inp["g_a_q"]), gsw(inp["g_a_k"]),
                          gsw(inp["g_b_q"]), gsw(inp["g_b_k"]), f(inp["lam_q1"]).reshape(-1),
                          f(inp["lam_k1"]).reshape(-1), f(inp["lam_q2"]).reshape(-1), f(inp["lam_k2"]).reshape(-1)])
    gbc = np.ascontiguousarray(np.broadcast_to(row[None, :], (128, row.size)))
    return gcols, gbc


_CACHE = {}


def run(xs_per_core, inp, seqs, debug=False, trace=False):
    key = (tuple(seqs), debug)
    if key not in _CACHE:
        _CACHE[key] = build_program(list(seqs), debug=debug)
    nc = _CACHE[key]
    cs, am, ident = host_consts(seqs)
    gcols, gbc = host_small(inp)
    f = lambda a: np.ascontiguousarray(np.asarray(a, np.float32))
    common = {
        "w_gu1": f(inp["w_ffn1_gu"][0]), "w_gu2": f(inp["w_ffn2_gu"][0]),
        "w_d1": f(inp["w_ffn1_down"][0]), "w_d2": f(inp["w_ffn2_down"][0]),
        "w_in": f(inp["w_in"][0]), "w_out": f(inp["w_out"][0]),
        "gcols": gcols, "gbc": gbc, "cs": cs, "ident": ident, "amask": am,
    }
    in_maps = [dict(common, x=f(x)) for x in xs_per_core]
    res = run_bass_kernel_spmd(nc, in_maps, core_ids=list(range(len(in_maps))), trace=trace)
    return res


def kernel(**inputs):
    xp = np.asarray(inputs["x_prompt"], np.float32)
    xsm = np.asarray(inputs["x_sample"], np.float32)
    n = 8
    seqs = (xp.shape[1], xsm.shape[1])
    xs = [np.concatenate([xp[c], xsm[c]], axis=0) for c in range(n)]
    res = run(xs, inputs, seqs)
    ys = [r["y"] for r in res.results]
    yp = np.stack([y[:seqs[0]] for y in ys], axis=0).astype(np.float32)
    ysm = np.stack([y[seqs[0]:] for y in ys], axis=0).astype(np.float32)
    return (yp, ysm)
```
